# Optimizing a Trainium2 kernel written in Bass

```python
import math
import jax
import jax.numpy as jnp
from jax import lax
import numpy as np

D_MODEL = 1024
BATCH = 16
SEQ = 4096
DEPTH = 4

GRID_W = 64
CTX_LEN = 256
D_RG = 1024
RG_HEADS = 16
RG_HEAD_DIM = D_RG // RG_HEADS
RG_CONV = 4
RG_PAD_L = 1
RG_PAD_R = 2
RG_C = 8.0
D_HG = 1024
HG_EXPAND = 128
HG_HEADS = D_HG // HG_EXPAND
HG_VDIM = D_HG // HG_HEADS
HG_CHUNK = 32
D_FF = 2816
FFN_CONV = 3
N_MOD = 6
EPS = 1e-6
O_RGX = 0
O_RGG = O_RGX + D_RG
O_HG = O_RGG + D_RG
O_GA = O_HG + 5 * D_HG
O_GB = O_GA + D_MODEL
IN_WIDTH = O_GB + D_MODEL

kernel_name = 'hybrid_rglru_hgrn2_convffn_prefix_dit'


def rms_norm(x, g):
    x32 = x.astype(jnp.float32)
    y = x32 * lax.rsqrt(jnp.mean(x32 * x32, axis=-1, keepdims=True) + EPS)
    return (y * g.astype(jnp.float32)).astype(x.dtype)


def dwconv(x, w, b, pad_l, pad_r):
    L = x.shape[1]
    xp = jnp.pad(x, ((0, 0), (pad_l, pad_r), (0, 0)))
    return b + sum(xp[:, j:j + L] * w[j] for j in range(w.shape[0]))


def to_col(t, rows):
    B, L = t.shape[:2]
    return t.reshape(B, rows, GRID_W, *t.shape[2:]).swapaxes(1, 2).reshape(B, L, *t.shape[2:])


def from_col(t, rows):
    B, L = t.shape[:2]
    return t.reshape(B, GRID_W, rows, *t.shape[2:]).swapaxes(1, 2).reshape(B, L, *t.shape[2:])


def linear_scan(a, b, h0, reverse):
    if reverse:
        a, b = jnp.flip(a, 1), jnp.flip(b, 1)
    comb = lambda l, r: (l[0] * r[0], r[0] * l[1] + r[1])
    A, Bc = lax.associative_scan(comb, (a, b), axis=1)
    h = A * h0[:, None] + Bc
    final = h[:, -1]
    if reverse:
        h = jnp.flip(h, 1)
    return h, final


def rglru_coeffs(x, wa, ba, wx, bx, lam):
    B, L, _ = x.shape
    xh = x.reshape(B, L, RG_HEADS, RG_HEAD_DIM)
    r = jax.nn.sigmoid(jnp.einsum('blhi,hij->blhj', xh, wa).reshape(B, L, D_RG) + ba)
    i = jax.nn.sigmoid(jnp.einsum('blhi,hij->blhj', xh, wx).reshape(B, L, D_RG) + bx)
    log_a = -RG_C * jax.nn.softplus(-lam) * r
    a = jnp.exp(log_a)
    b = jnp.sqrt(-jnp.expm1(2.0 * log_a)) * (i * x)
    return a, b


def hgrn2_scan(q, logf, k, v, s0):
    B, L = q.shape[:2]
    n = L // HG_CHUNK
    chunks = lambda t: t.reshape(B, n, HG_CHUNK, *t.shape[2:]).swapaxes(0, 1)
    tri = jnp.tril(jnp.ones((HG_CHUNK, HG_CHUNK), dtype=bool))[None, :, :, None, None]

    def step(S, inp):
        qc, gc, kc, vc = inp
        bcum = jnp.cumsum(gc, axis=1)
        o_inter = jnp.einsum('bchk,bhkv->bchv', qc * jnp.exp(bcum), S)
        diff = bcum[:, :, None] - bcum[:, None, :]
        w = jnp.exp(jnp.where(tri, diff, -jnp.inf))
        A = jnp.einsum('bthk,btshk,bshk->bhts', qc, w, kc)
        o_intra = jnp.einsum('bhts,bshv->bthv', A, vc)
        blast = bcum[:, -1]
        S_new = jnp.exp(blast)[..., None] * S + jnp.einsum('bshk,bshv->bhkv', kc * jnp.exp(blast[:, None] - bcum), vc)
        return S_new.astype(S.dtype), o_inter + o_intra

    S_fin, o = lax.scan(step, s0, (chunks(q), chunks(logf), chunks(k), chunks(v)))
    return o.swapaxes(0, 1).reshape(B, L, HG_HEADS, HG_VDIM), S_fin


def hgrn2_dir(q, logf, k, v, s0, reverse):
    if reverse:
        q, logf, k, v = (jnp.flip(t, 1) for t in (q, logf, k, v))
    o, S = hgrn2_scan(q, logf, k, v, s0)
    if reverse:
        o = jnp.flip(o, 1)
    return o, S


def hg_features(zb, lb):
    B, L, _ = zb.shape
    hs = lambda t: t.reshape(B, L, HG_HEADS, HG_EXPAND)
    lb = lb.reshape(HG_HEADS, HG_EXPAND).astype(zb.dtype)
    q = hs(jax.nn.silu(zb[..., 0:D_HG]))
    dirs = []
    for j in (1, 2):
        zf = hs(zb[..., j * D_HG:(j + 1) * D_HG])
        logf = jnp.logaddexp(jnp.log(lb), jnp.log1p(-lb) + jax.nn.log_sigmoid(zf))
        k = (1.0 - lb) * jax.nn.sigmoid(-zf)
        dirs.append((logf, k))
    v = zb[..., 3 * D_HG:4 * D_HG].reshape(B, L, HG_HEADS, HG_VDIM)
    og = zb[..., 4 * D_HG:5 * D_HG]
    return q, dirs, v, og


def hg_out(o, og, gain):
    B, L = o.shape[:2]
    return rms_norm(o, gain.reshape(HG_HEADS, HG_VDIM)).reshape(B, L, D_HG) * jax.nn.silu(og)


def merge(z, r, y_hg, lp):
    y_rg = jax.nn.gelu(z[..., O_RGG:O_RGG + D_RG]) * r
    p_rg = y_rg @ lp['w_proj_rg']
    p_hg = y_hg @ lp['w_proj_hg']
    m = jax.nn.sigmoid(z[..., O_GA:O_GA + D_MODEL]) * p_rg + jax.nn.sigmoid(z[..., O_GB:O_GB + D_MODEL]) * p_hg
    return m @ lp['w_out']


def mixer(hc, hl, lp, lb, rows, need_ctx):
    B = hl.shape[0]
    zc = hc @ lp['w_in']
    zl = hl @ lp['w_in']
    xc = dwconv(zc[..., O_RGX:O_RGX + D_RG], lp['rg_conv_w'], lp['rg_conv_b'], RG_PAD_L, RG_PAD_R)
    xl = dwconv(zl[..., O_RGX:O_RGX + D_RG], lp['rg_conv_w'], lp['rg_conv_b'], RG_PAD_L, RG_PAD_R)
    rc, rl = 0.0, 0.0
    for d in range(2):
        prm = (lp['rg_wa'][d], lp['rg_ba'][d], lp['rg_wx'][d], lp['rg_bx'][d], lp['rg_lam'][d])
        rev = d == 1
        hcd, st = linear_scan(*rglru_coeffs(xc, *prm), jnp.zeros((B, D_RG), xc.dtype), rev)
        hld, _ = linear_scan(*rglru_coeffs(xl, *prm), st, rev)
        rc = rc + hcd
        rl = rl + hld
    qc, dirs_c, vc, ogc = hg_features(zc[..., O_HG:O_HG + 5 * D_HG], lb)
    ql, dirs_l, vl, ogl = hg_features(to_col(zl[..., O_HG:O_HG + 5 * D_HG], rows), lb)
    oc, ol = 0.0, 0.0
    for d in range(2):
        rev = d == 1
        s0 = jnp.zeros((B, HG_HEADS, HG_EXPAND, HG_VDIM), vc.dtype)
        ocd, st = hgrn2_dir(qc, dirs_c[d][0], dirs_c[d][1], vc, s0, rev)
        old, _ = hgrn2_dir(ql, dirs_l[d][0], dirs_l[d][1], vl, st, rev)
        oc = oc + ocd
        ol = ol + old
    yl = merge(zl, rl, from_col(hg_out(ol, ogl, lp['hg_out_norm']), rows), lp)
    yc = merge(zc, rc, hg_out(oc, ogc, lp['hg_out_norm']), lp) if need_ctx else None
    return yc, yl


def conv_ffn(h, w_up, cw, cb, w_down, rows):
    B, L, _ = h.shape
    u = h @ w_up
    g, v = u[..., :D_FF], u[..., D_FF:]
    if rows is None:
        g = dwconv(g, cw, cb, 1, 1)
    else:
        g = dwconv(g.reshape(B * rows, GRID_W, D_FF), cw, cb, 1, 1).reshape(B, L, D_FF)
    return (jax.nn.silu(g) * v) @ w_down


def setup_inputs(seed: int = 0) -> dict:
    key = jax.random.key(seed)
    ks = iter(jax.random.split(key, 40))
    f32 = jnp.float32
    nrm = lambda shape, fan_in: jax.random.normal(next(ks), shape, f32) * fan_in ** -0.5
    small = lambda shape: 0.01 * jax.random.normal(next(ks), shape, f32)
    gain = lambda shape: 1.0 + 0.02 * jax.random.normal(next(ks), shape, f32)
    x = jax.random.normal(next(ks), (BATCH, SEQ, D_MODEL), f32)
    c = jax.random.normal(next(ks), (BATCH, D_MODEL), f32)
    ctx = jax.random.normal(next(ks), (BATCH, CTX_LEN, D_MODEL), f32)
    c_ctx = jax.random.normal(next(ks), (D_MODEL,), f32)
    w_ada = 0.5 * nrm((DEPTH, D_MODEL, N_MOD * D_MODEL), D_MODEL)
    b_ada = small((DEPTH, N_MOD * D_MODEL))
    g_pre_mix = gain((DEPTH, D_MODEL))
    g_post_mix = gain((DEPTH, D_MODEL))
    g_pre_ffn = gain((DEPTH, D_MODEL))
    g_post_ffn = gain((DEPTH, D_MODEL))
    w_in = nrm((DEPTH, D_MODEL, IN_WIDTH), D_MODEL)
    rg_conv_w = nrm((DEPTH, RG_CONV, D_RG), RG_CONV)
    rg_conv_b = small((DEPTH, D_RG))
    rg_wa = nrm((DEPTH, 2, RG_HEADS, RG_HEAD_DIM, RG_HEAD_DIM), RG_HEAD_DIM)
    rg_ba = small((DEPTH, 2, D_RG))
    rg_wx = nrm((DEPTH, 2, RG_HEADS, RG_HEAD_DIM, RG_HEAD_DIM), RG_HEAD_DIM)
    rg_bx = small((DEPTH, 2, D_RG))
    u = jax.random.uniform(next(ks), (DEPTH, 2, D_RG), f32, minval=0.9, maxval=0.999)
    a0 = u ** (1.0 / RG_C)
    rg_lam = jnp.log(a0) - jnp.log1p(-a0)
    hg_lb_logits = 0.1 * jax.random.normal(next(ks), (DEPTH, D_HG), f32)
    hg_out_norm = gain((DEPTH, D_HG))
    w_proj_rg = nrm((DEPTH, D_RG, D_MODEL), D_RG)
    w_proj_hg = nrm((DEPTH, D_HG, D_MODEL), D_HG)
    w_out = nrm((DEPTH, D_MODEL, D_MODEL), D_MODEL)
    ffn_w_up = nrm((DEPTH, D_MODEL, 2 * D_FF), D_MODEL)
    ffn_conv_w = nrm((DEPTH, FFN_CONV, D_FF), FFN_CONV)
    ffn_conv_b = small((DEPTH, D_FF))
    ffn_w_down = nrm((DEPTH, D_FF, D_MODEL), D_FF)
    return {'x': x, 'c': c, 'ctx': ctx, 'c_ctx': c_ctx, 'w_ada': w_ada, 'b_ada': b_ada,
            'g_pre_mix': g_pre_mix, 'g_post_mix': g_post_mix, 'g_pre_ffn': g_pre_ffn, 'g_post_ffn': g_post_ffn,
            'w_in': w_in, 'rg_conv_w': rg_conv_w, 'rg_conv_b': rg_conv_b, 'rg_wa': rg_wa, 'rg_ba': rg_ba,
            'rg_wx': rg_wx, 'rg_bx': rg_bx, 'rg_lam': rg_lam, 'hg_lb_logits': hg_lb_logits,
            'hg_out_norm': hg_out_norm, 'w_proj_rg': w_proj_rg, 'w_proj_hg': w_proj_hg, 'w_out': w_out,
            'ffn_w_up': ffn_w_up, 'ffn_conv_w': ffn_conv_w, 'ffn_conv_b': ffn_conv_b, 'ffn_w_down': ffn_w_down}


def reference(x, c, ctx, c_ctx, w_ada, b_ada, g_pre_mix, g_post_mix, g_pre_ffn, g_post_ffn,
              w_in, rg_conv_w, rg_conv_b, rg_wa, rg_ba, rg_wx, rg_bx, rg_lam, hg_lb_logits,
              hg_out_norm, w_proj_rg, w_proj_hg, w_out, ffn_w_up, ffn_conv_w, ffn_conv_b, ffn_w_down):
    B, L, _ = x.shape
    rows = L // GRID_W
    p = jax.nn.softmax(hg_lb_logits.astype(jnp.float32), axis=0)
    cum = jnp.cumsum(p, axis=0)
    lb_all = cum - cum[0:1]
    sc = jax.nn.silu(c)
    scc = jax.nn.silu(c_ctx)
    xl, xc = x, ctx
    for l in range(DEPTH):
        need_ctx = l < DEPTH - 1
        ml = jnp.split((sc @ w_ada[l] + b_ada[l])[:, None, :], N_MOD, axis=-1)
        mc = jnp.split(scc @ w_ada[l] + b_ada[l], N_MOD, axis=-1)
        lp = dict(w_in=w_in[l], rg_conv_w=rg_conv_w[l], rg_conv_b=rg_conv_b[l], rg_wa=rg_wa[l],
                  rg_ba=rg_ba[l], rg_wx=rg_wx[l], rg_bx=rg_bx[l], rg_lam=rg_lam[l],
                  hg_out_norm=hg_out_norm[l], w_proj_rg=w_proj_rg[l], w_proj_hg=w_proj_hg[l], w_out=w_out[l])
        hl = rms_norm(xl, g_pre_mix[l]) * (1.0 + ml[1]) + ml[0]
        hc = rms_norm(xc, g_pre_mix[l]) * (1.0 + mc[1]) + mc[0]
        yc, yl = mixer(hc, hl, lp, lb_all[l], rows, need_ctx)
        xl = xl + ml[2] * rms_norm(yl, g_post_mix[l])
        hl = rms_norm(xl, g_pre_ffn[l]) * (1.0 + ml[4]) + ml[3]
        xl = xl + ml[5] * rms_norm(conv_ffn(hl, ffn_w_up[l], ffn_conv_w[l], ffn_conv_b[l], ffn_w_down[l], rows), g_post_ffn[l])
        if need_ctx:
            xc = xc + mc[2] * rms_norm(yc, g_post_mix[l])
            hc = rms_norm(xc, g_pre_ffn[l]) * (1.0 + mc[4]) + mc[3]
            xc = xc + mc[5] * rms_norm(conv_ffn(hc, ffn_w_up[l], ffn_conv_w[l], ffn_conv_b[l], ffn_w_down[l], None), g_post_ffn[l])
    return xl
```

```python
import contextlib
import numpy as np
import concourse.bass as bass
import concourse.mybir as mybir
from concourse.bass_utils import run_bass_kernel_spmd

F32 = mybir.dt.float32
BF16 = mybir.dt.bfloat16
AF = mybir.ActivationFunctionType
ALU = mybir.AluOpType

ENGS = ("pe", "act", "dve", "pool", "sp")

NCORES = 8
NB = 2
DEPTH = 4
D = 1024
KC = 8
CTX = 256
LAT = 4096
NT = CTX + LAT
GW = 64
DFF = 2816
FFC = 22
NCB = 72
EPS = 1e-6
ARENA_BYTES = 211968
TILES512 = [(0, 256)] + [(256 + 512 * i, 512) for i in range(8)]
TILES256 = [(256 * i, 256) for i in range(17)]
CB_RGX, CB_RGG, CB_Q, CB_FF, CB_FB, CB_V, CB_OG, CB_GA, CB_GB = 0, 8, 16, 24, 32, 40, 48, 56, 64


def vec_layout():
    ent = []
    for l in range(DEPTH):
        for n in ("g_pre_mix", "g_post_mix", "g_pre_ffn", "g_post_ffn"):
            ent.append((f"{n}{l}", 8))
        for j in range(4):
            ent.append((f"rg_cw{l}_{j}", 8))
        ent.append((f"rg_cb{l}", 8))
        for d in range(2):
            ent.append((f"rg_ba{l}_{d}", 8))
            ent.append((f"rg_bx{l}_{d}", 8))
        ent.append((f"hg_gain{l}", 8))
        for j in range(3):
            ent.append((f"ffn_cw{l}_{j}", FFC))
        ent.append((f"ffn_cb{l}", FFC))
        ent.append((f"b_ada{l}", 48))
    ent.append(("rg_lam", 64))
    ent.append(("hg_lb", 32))
    ent.append(("c3", 24))
    off = {}
    o = 0
    for n, k in ent:
        off[n] = (o, k)
        o += k
    return off, o


VOFF, NV = vec_layout()


class Dep:
    __slots__ = ("w", "r")

    def __init__(self):
        self.w = {}
        self.r = {}


class Tl:
    __slots__ = ("ap", "dep")

    def __init__(self, ap, dep=None):
        self.ap = ap
        self.dep = dep if dep is not None else Dep()


class Prog:
    def __init__(self, nc, n_dma_sems=(("sp", 12), ("pool", 8))):
        self.nc = nc
        self.es = contextlib.ExitStack()
        self.prog = {e: [] for e in ENGS}
        self.sems = {}
        self.cnt = {}
        self.waited = {e: {} for e in ENGS}
        for e in ENGS:
            self.sems[e] = self.es.enter_context(nc.semaphore("c_" + e))
            self.cnt[e] = 0
        self.dma_pool = {}
        for q, n in n_dma_sems:
            lst = []
            for i in range(n):
                key = "d_%s%d" % (q, i)
                self.sems[key] = self.es.enter_context(nc.semaphore(key))
                self.cnt[key] = 0
                lst.append(key)
            self.dma_pool[q] = [lst, 0]
        self.n_inst = 0

    def _waits(self, eng, reads, writes, extra=()):
        need = {}
        for d in reads:
            for k, v in d.w.items():
                if need.get(k, 0) < v:
                    need[k] = v
        for d in writes:
            for k, v in d.w.items():
                if need.get(k, 0) < v:
                    need[k] = v
            for k, v in d.r.items():
                if need.get(k, 0) < v:
                    need[k] = v
        for k, v in extra:
            if need.get(k, 0) < v:
                need[k] = v
        out = []
        wd = self.waited[eng]
        for k, v in need.items():
            if k == eng and eng == "pe":
                continue
            if wd.get(k, 0) >= v:
                continue
            wd[k] = v
            out.append((k, v))
        return out

    def _mark(self, ev, reads, writes):
        k, v = ev
        for d in reads:
            if d.r.get(k, 0) < v:
                d.r[k] = v
        for d in writes:
            if d.w.get(k, 0) < v:
                d.w[k] = v

    def op(self, eng, fn, reads=(), writes=(), signal=True):
        waits = self._waits(eng, reads, writes)
        if signal:
            self.cnt[eng] += 1
            ev = (eng, self.cnt[eng])
        else:
            ev = (eng, self.cnt[eng] + 1)
        sems = self.sems
        semE = sems[eng]

        def emit(e, waits=waits, fn=fn, signal=signal):
            for k, v in waits:
                e.wait_ge(sems[k], v)
            ins = fn(e)
            if signal:
                ins.then_inc(semE, 1)
        self.prog[eng].append(emit)
        self._mark(ev, reads, writes)
        self.n_inst += 1
        return ev

    def dma(self, q, out, in_, reads=(), writes=(), **kw):
        lst, idx = self.dma_pool[q]
        key = lst[idx % len(lst)]
        self.dma_pool[q][1] = idx + 1
        extra = ((key, self.cnt[key]),) if self.cnt[key] else ()
        waits = self._waits(q, reads, writes, extra=extra)
        self.cnt[key] += 16
        ev = (key, self.cnt[key])
        sems = self.sems

        def emit(e, waits=waits, out=out, in_=in_, kw=kw, key=key):
            for k, v in waits:
                e.wait_ge(sems[k], v)
            e.dma_start(out=out, in_=in_, **kw).then_inc(sems[key], 16)
        self.prog[q].append(emit)
        self._mark(ev, reads, writes)
        self.n_inst += 1
        return ev

    def barrier(self):
        need = [(k, v) for k, v in self.cnt.items() if v > 0]
        sems = self.sems
        for eng in ENGS:
            wd = self.waited[eng]
            ws = []
            for k, v in need:
                if wd.get(k, 0) >= v:
                    continue
                wd[k] = v
                ws.append((k, v))

            def emit(e, ws=ws):
                for k, v in ws:
                    e.wait_ge(sems[k], v)
            self.prog[eng].append(emit)

    def finish(self):
        nc = self.nc
        prog = self.prog
        with nc.Block() as block:
            @block.tensor
            def _(e):
                for f in prog["pe"]:
                    f(e)

            @block.scalar
            def _(e):
                for f in prog["act"]:
                    f(e)

            @block.vector
            def _(e):
                for f in prog["dve"]:
                    f(e)

            @block.gpsimd
            def _(e):
                for f in prog["pool"]:
                    f(e)

            @block.sync
            def _(e):
                for f in prog["sp"]:
                    f(e)
        self.es.close()


class Arena:
    def __init__(self, P, nbytes):
        self.P = P
        self.t = P.es.enter_context(P.nc.sbuf_tensor("arena", [128, nbytes // 4], F32))
        self.nbytes = nbytes
        self.off = 0

    def mark(self):
        return self.off

    def reset(self, m):
        self.off = m

    def alloc(self, shape, dt):
        n = 1
        for s in shape:
            n *= s
        esz = 4 if dt == F32 else 2
        nb = (n * esz + 63) // 64 * 64
        assert self.off + nb <= self.nbytes, ("arena overflow", self.off, nb, self.nbytes)
        v = self.t[:, self.off // 4:(self.off + nb) // 4]
        self.off += nb
        if dt != F32:
            v = v.bitcast(dt)
        v = v[:, 0:n]
        if len(shape) == 2:
            v = v.rearrange("p (a b) -> p a b", a=shape[0])
        elif len(shape) == 3:
            v = v.rearrange("p (a b c) -> p a b c", a=shape[0], b=shape[1])
        return Tl(v)


class Builder:
    def __init__(self, nb=NB, layers=range(DEPTH), phases="PRHCD", dbg=()):
        self.nb = nb
        self.layers = list(layers)
        self.phases = phases
        self.dbg = set(dbg)
        nc = bass.Bass("TRN2", target_bir_lowering=False)
        self.nc = nc

        def I(name, shape, dt=F32):
            return nc.dram_tensor(name, list(shape), dt, kind="ExternalInput").ap()

        def S(name, shape, dt):
            kind = "ExternalOutput" if name in self.dbg else "Internal"
            return nc.dram_tensor(name, list(shape), dt, kind=kind).ap()

        self.xin = I("xin", [nb, KC, 128, NT])
        self.vecs_d = I("vecs", [128, NV])
        self.w_ada = I("w_ada", [DEPTH, 6, 128, KC, D])
        self.w_in = I("w_in", [DEPTH, NCB, 128, KC * 128])
        self.rg_wa = I("rg_wa", [DEPTH, 2, 16, 64, 64])
        self.rg_wx = I("rg_wx", [DEPTH, 2, 16, 64, 64])
        self.w_prg = I("w_prg", [DEPTH, 128, KC, D])
        self.w_phg = I("w_phg", [DEPTH, 128, KC, D])
        self.w_out = I("w_out", [DEPTH, 128, KC, D])
        self.w_up = I("w_up", [DEPTH, 128, KC, 2 * DFF])
        self.w_dn = I("w_dn", [DEPTH, 128, FFC, D])
        self.out = nc.dram_tensor("out", [nb, KC, 128, LAT], F32, kind="ExternalOutput").ap()
        self.zs = S("zs", [nb, NCB, 128, NT], BF16)
        self.yrg = S("yrg", [nb, KC, 128, NT], BF16)
        self.yhg = S("yhg", [nb, KC, 128, NT], BF16)
        self.xs = S("xs", [nb, KC, 128, NT], F32)
        self.dbg_out = {}
        self.P = Prog(nc)
        self.A = Arena(self.P, ARENA_BYTES)
        self.banks = [Tl(self.P.es.enter_context(nc.psum_tensor("bank%d" % i, [128, 512], F32))[:]) for i in range(8)]
        self.bank_rr = 0

    def act(self, out, in_, func, reads, writes, scale=1.0, bias=None):
        if bias is None:
            fn = lambda e: e.activation(out=out, in_=in_, func=func, scale=scale)
        else:
            fn = lambda e: e.activation(out=out, in_=in_, func=func, scale=scale, bias=bias)
        return self.P.op("act", fn, reads, writes)

    def tt(self, eng, out, in0, in1, op, reads, writes):
        return self.P.op(eng, lambda e: e.tensor_tensor(out=out, in0=in0, in1=in1, op=op), reads, writes)

    def ts(self, eng, out, in0, s1, s2, op0, op1, reads, writes):
        if s2 is None:
            fn = lambda e: e.tensor_scalar(out=out, in0=in0, scalar1=s1, scalar2=None, op0=op0)
        else:
            fn = lambda e: e.tensor_scalar(out=out, in0=in0, scalar1=s1, scalar2=s2, op0=op0, op1=op1)
        return self.P.op(eng, fn, reads, writes)

    def stt(self, out, in0, scalar, in1, op0, op1, reads, writes):
        return self.P.op("dve", lambda e: e.scalar_tensor_tensor(out=out, in0=in0, scalar=scalar, in1=in1, op0=op0, op1=op1), reads, writes)

    def cp(self, eng, out, in_, reads, writes):
        return self.P.op(eng, lambda e: e.tensor_copy(out=out, in_=in_), reads, writes)

    def mm(self, out, lhsT, rhs, start, stop, reads, writes, signal=True):
        return self.P.op("pe", lambda e: e.matmul(out, lhsT=lhsT, rhs=rhs, start=start, stop=stop), reads, writes, signal=signal)

    def tr(self, out, in_, reads, writes, signal=True):
        ident = self.ident.ap
        return self.P.op("pe", lambda e: e.transpose(out, in_, ident), list(reads) + [self.ident.dep], writes, signal=signal)

    def memset(self, eng, ap, val, writes):
        return self.P.op(eng, lambda e: e.memset(ap, val), (), writes)

    def vcol(self, name, k=None):
        o, n = VOFF[name]
        if k is None:
            return self.vecs.ap[:, o:o + n]
        return self.vecs.ap[:, o + k:o + k + 1]

    def next_bank(self, lo=0, hi=4):
        b = self.banks[lo + self.bank_rr % (hi - lo)]
        self.bank_rr += 1
        return b

    def phase0(self):
        P, A = self.P, self.A
        self.vecs = A.alloc([NV], F32)
        P.dma("sp", self.vecs.ap, self.vecs_d, writes=[self.vecs.dep])
        self.ident = A.alloc([128], BF16)
        self.ones = A.alloc([128], BF16)
        self.m2f = A.alloc([128], BF16)
        self.m2b = A.alloc([128], BF16)
        self.cst = A.alloc([8], F32)
        self.memset("pool", self.cst.ap[:, 0:1], EPS, [self.cst.dep])
        self.memset("pool", self.cst.ap[:, 2:5], 0.0, [self.cst.dep])
        for h in range(3):
            self.memset("pool", self.cst.ap[32 * h:32 * h + 32, 2 + h:3 + h], 1.0, [self.cst.dep])
        self.memset("pool", self.cst.ap[:, 1:2], 1.0, [self.cst.dep])
        self.memset("pool", self.ident.ap, 0.0, [self.ident.dep])
        idap = self.ident.ap
        P.op("pool", lambda e: e.affine_select(out=idap, in_=idap, pattern=[[-1, 128]], compare_op=ALU.not_equal, fill=1.0, base=0, channel_multiplier=1),
             [self.ident.dep], [self.ident.dep])
        self.memset("pool", self.ones.ap, 1.0, [self.ones.dep])
        for m, mult, pat in ((self.m2f, -1, 1), (self.m2b, 1, -1)):
            self.memset("pool", m.ap, 1.0, [m.dep])
            map_ = m.ap
            P.op("pool", lambda e, map_=map_, mult=mult, pat=pat: e.affine_select(out=map_, in_=map_, pattern=[[pat, 128]], compare_op=ALU.is_ge, fill=0.0, base=0, channel_multiplier=mult),
                 [m.dep], [m.dep])
        self.memset("pool", self.m2f.ap[0:32, 32:128], 0.0, [self.m2f.dep])
        self.memset("pool", self.m2f.ap[32:64, 64:128], 0.0, [self.m2f.dep])
        self.memset("pool", self.m2b.ap[32:64, 0:32], 0.0, [self.m2b.dep])
        self.memset("pool", self.m2b.ap[64:128, 0:64], 0.0, [self.m2b.dep])
        self.modraw = A.alloc([DEPTH * 6, KC, 3], F32)
        self.sc1 = A.alloc([DEPTH, KC, 3], F32)
        self.gt1 = A.alloc([DEPTH, KC, 3], F32)
        self.sc2 = A.alloc([DEPTH, KC, 3], F32)
        self.gt2 = A.alloc([DEPTH, KC, 3], F32)
        self.lbv = A.alloc([DEPTH, KC], F32)
        self.oml = A.alloc([DEPTH, KC], F32)
        self.noml = A.alloc([DEPTH, KC], F32)
        self.cl = A.alloc([DEPTH * 2, KC], F32)
        self.cl2 = A.alloc([DEPTH * 2, KC], F32)
        mark = A.mark()
        scv = A.alloc([KC, 3], F32)
        o, n = VOFF["c3"]
        self.act(scv.ap, self.vecs.ap[:, o:o + n].rearrange("p (k w) -> p k w", w=3), AF.Silu, [self.vecs.dep], [scv.dep])
        wbuf = [A.alloc([KC, D], F32) for _ in range(2)]
        it = 0
        for l in range(DEPTH):
            for j in range(6):
                wb = wbuf[it % 2]
                P.dma("sp", wb.ap, self.w_ada[l, j], writes=[wb.dep])
                bank = self.banks[it % 2]
                for oc in range(KC):
                    for kc in range(KC):
                        self.mm(bank.ap[:, oc * 4:oc * 4 + 3], wb.ap[:, kc, oc * 128:(oc + 1) * 128], scv.ap[:, kc, :],
                                kc == 0, kc == KC - 1, [wb.dep, scv.dep], [bank.dep], signal=(kc == KC - 1))
                ob, _ = VOFF[f"b_ada{l}"]
                bia = self.vecs.ap[:, ob + j * 8:ob + j * 8 + 8].unsqueeze(2).broadcast_to([128, KC, 3])
                self.tt("dve", self.modraw.ap[:, l * 6 + j], bank.ap[:, 0:32].rearrange("p (k f) -> p k f", f=4)[:, :, 0:3], bia, ALU.add,
                        [bank.dep, self.vecs.dep], [self.modraw.dep])
                it += 1
        tmp = A.alloc([KC, 3], F32)
        for l in range(DEPTH):
            for (dst, jscale, gname) in ((self.sc1, 1, "g_pre_mix"), (self.sc2, 4, "g_pre_ffn")):
                g = self.vcol(f"{gname}{l}").unsqueeze(2).broadcast_to([128, KC, 3])
                self.ts("dve", tmp.ap, self.modraw.ap[:, l * 6 + jscale], 1.0, None, ALU.add, None, [self.modraw.dep], [tmp.dep])
                self.tt("dve", dst.ap[:, l], tmp.ap, g, ALU.mult, [tmp.dep, self.vecs.dep], [dst.dep])
            for (dst, jg, gname) in ((self.gt1, 2, "g_post_mix"), (self.gt2, 5, "g_post_ffn")):
                g = self.vcol(f"{gname}{l}").unsqueeze(2).broadcast_to([128, KC, 3])
                self.tt("dve", dst.ap[:, l], self.modraw.ap[:, l * 6 + jg], g, ALU.mult, [self.modraw.dep, self.vecs.dep], [dst.dep])
        ex = A.alloc([DEPTH, KC], F32)
        sm = A.alloc([KC], F32)
        self.act(ex.ap, self.vcol("hg_lb").rearrange("p (l k) -> p l k", k=KC), AF.Exp, [self.vecs.dep], [ex.dep])
        self.tt("dve", sm.ap, ex.ap[:, 0], ex.ap[:, 1], ALU.add, [ex.dep], [sm.dep])
        self.tt("dve", sm.ap, sm.ap, ex.ap[:, 2], ALU.add, [ex.dep, sm.dep], [sm.dep])
        self.tt("dve", sm.ap, sm.ap, ex.ap[:, 3], ALU.add, [ex.dep, sm.dep], [sm.dep])
        smap = sm.ap
        P.op("dve", lambda e: e.reciprocal(out=smap, in_=smap), [sm.dep], [sm.dep])
        self.tt("dve", ex.ap, ex.ap, sm.ap.unsqueeze(1).broadcast_to([128, DEPTH, KC]), ALU.mult, [ex.dep, sm.dep], [ex.dep])
        self.memset("dve", self.lbv.ap[:, 0], 0.0, [self.lbv.dep])
        self.cp("dve", self.lbv.ap[:, 1], ex.ap[:, 1], [ex.dep], [self.lbv.dep])
        self.tt("dve", self.lbv.ap[:, 2], self.lbv.ap[:, 1], ex.ap[:, 2], ALU.add, [ex.dep, self.lbv.dep], [self.lbv.dep])
        self.tt("dve", self.lbv.ap[:, 3], self.lbv.ap[:, 2], ex.ap[:, 3], ALU.add, [ex.dep, self.lbv.dep], [self.lbv.dep])
        self.ts("dve", self.oml.ap, self.lbv.ap, -1.0, 1.0, ALU.mult, ALU.add, [self.lbv.dep], [self.oml.dep])
        self.ts("dve", self.noml.ap, self.lbv.ap, -1.0, None, ALU.add, None, [self.lbv.dep], [self.noml.dep])
        e_ = A.alloc([DEPTH * 2, KC], F32)
        s_ = A.alloc([DEPTH * 2, KC], F32)
        l1 = A.alloc([DEPTH * 2, KC], F32)
        mk = A.alloc([DEPTH * 2, KC], F32)
        lam = self.vcol("rg_lam").rearrange("p (a k) -> p a k", k=KC)
        self.act(e_.ap, lam, AF.Exp, [self.vecs.dep], [e_.dep], scale=-1.0)
        self.act(l1.ap, e_.ap, AF.Ln, [e_.dep, self.cst.dep], [l1.dep], bias=self.cst.ap[:, 1:2])
        self.ts("dve", s_.ap, e_.ap, -0.2, 0.25, ALU.mult, ALU.add, [e_.dep], [s_.dep])
        for cst in (1.0 / 3, 0.5, 1.0):
            self.tt("dve", s_.ap, s_.ap, e_.ap, ALU.mult, [s_.dep, e_.dep], [s_.dep])
            self.ts("dve", s_.ap, s_.ap, -1.0, cst, ALU.mult, ALU.add, [s_.dep], [s_.dep])
        self.tt("dve", s_.ap, s_.ap, e_.ap, ALU.mult, [s_.dep, e_.dep], [s_.dep])
        self.ts("dve", mk.ap, e_.ap, 0.03, None, ALU.is_lt, None, [e_.dep], [mk.dep])
        self.tt("dve", s_.ap, s_.ap, l1.ap, ALU.subtract, [s_.dep, l1.dep], [s_.dep])
        self.tt("dve", s_.ap, s_.ap, mk.ap, ALU.mult, [s_.dep, mk.dep], [s_.dep])
        self.tt("dve", s_.ap, s_.ap, l1.ap, ALU.add, [s_.dep, l1.dep], [s_.dep])
        self.ts("dve", self.cl.ap, s_.ap, -8.0, None, ALU.mult, None, [s_.dep], [self.cl.dep])
        self.ts("dve", self.cl2.ap, s_.ap, -16.0, None, ALU.mult, None, [s_.dep], [self.cl2.dep])
        P.barrier()
        A.reset(mark)
        self.base_mark = mark

    def rstd_from_ss(self, ss_bank, n, inv_n, rt, rstd):
        self.act(rt.ap[:, :n], ss_bank.ap[:, :n], AF.Sqrt, [ss_bank.dep, self.cst.dep], [rt.dep], scale=inv_n, bias=self.cst.ap[:, 0:1])
        ra, oa = rt.ap[:, :n], rstd.ap[:, :n]
        self.P.op("dve", lambda e: e.reciprocal(out=oa, in_=ra), [rt.dep], [rstd.dep])

    def norm_tile(self, src, b, t0, n, w, l, sc, shift_j, xt, hb, sq, rt, rstd, tmps, ssb):
        P = self.P
        P.dma("sp", xt.ap[:, :, :n], src[b].rearrange("k p t -> p k t")[:, :, t0:t0 + n], writes=[xt.dep])
        for kc in range(KC):
            s = sq[kc % 2]
            self.act(s.ap[:, :n], xt.ap[:, kc, :n], AF.Square, [xt.dep], [s.dep])
            self.mm(ssb.ap[:, :n], self.ones.ap, s.ap[:, :n], kc == 0, kc == KC - 1, [self.ones.dep, s.dep], [ssb.dep])
        self.rstd_from_ss(ssb, n, 1.0 / D, rt, rstd)
        for kc in range(KC):
            t = tmps[kc % 2]
            self.stt(t.ap[:, :n], xt.ap[:, kc, :n], sc.ap[:, l, kc, w:w + 1], rstd.ap[:, :n], ALU.mult, ALU.mult,
                     [xt.dep, sc.dep, rstd.dep], [t.dep])
            self.act(hb.ap[:, kc, :n], t.ap[:, :n], AF.Identity, [t.dep, self.modraw.dep], [hb.dep],
                     bias=self.modraw.ap[:, l * 6 + shift_j, kc, w:w + 1])

    def tiles(self, tl, skip_ctx=False):
        out = []
        for b in range(self.nb):
            for (t0, n) in tl:
                if skip_ctx and t0 < CTX:
                    continue
                out.append((b, t0, n, 2 if t0 < CTX else b))
        return out

    def phase_proj(self, l, src):
        P, A = self.P, self.A
        mark = A.mark()
        W = A.alloc([NCB, KC, 128], BF16)
        wdeps = [Dep() for _ in range(NCB)]
        order = (list(range(CB_Q, CB_Q + 8)) + list(range(CB_OG, CB_OG + 8)) +
                 list(range(CB_RGX, CB_RGX + 16)) + list(range(CB_V, CB_V + 8)) +
                 list(range(CB_FF, CB_FF + 16)) + list(range(CB_GA, CB_GA + 16)))
        for cb in order:
            P.dma("pool", W.ap[:, cb].rearrange("p k j -> p (k j)"), self.w_in[l, cb], writes=[wdeps[cb]])
        xt = A.alloc([KC, 512], F32)
        hbs = [A.alloc([KC, 512], BF16) for _ in range(2)]
        sq = [A.alloc([512], BF16) for _ in range(2)]
        rt = A.alloc([512], F32)
        rstd = A.alloc([512], F32)
        tmps = [A.alloc([512], F32) for _ in range(2)]
        evs = [A.alloc([2, 512], BF16) for _ in range(4)]
        ssb = self.banks[4]
        tiles = self.tiles(TILES512)

        def kind(cb):
            if CB_Q <= cb < CB_Q + 8 or CB_OG <= cb < CB_OG + 8:
                return "silu"
            if cb < 16 or CB_V <= cb < CB_V + 8:
                return "copy"
            return "sig"

        def norm(i):
            b, t0, n, w = tiles[i]
            self.norm_tile(src, b, t0, n, w, l, self.sc1, 0, xt, hbs[i % 2], sq, rt, rstd, tmps, ssb)
        norm(0)
        evi = 0
        for i, (b, t0, n, w) in enumerate(tiles):
            hb = hbs[i % 2]
            for bi, cb in enumerate(order):
                if bi == 16 and i + 1 < len(tiles):
                    norm(i + 1)
                bank = self.next_bank(0, 4)
                for kc in range(KC):
                    self.mm(bank.ap[:, :n], W.ap[:, cb, kc, :], hb.ap[:, kc, :n], kc == 0, kc == KC - 1,
                            [wdeps[cb], hb.dep], [bank.dep], signal=(kc == KC - 1))
                ev = evs[(evi // 2) % 4]
                g = evi % 2
                k = kind(cb)
                if k == "silu":
                    self.act(ev.ap[:, g, :n], bank.ap[:, :n], AF.Silu, [bank.dep], [ev.dep])
                elif k == "sig":
                    self.act(ev.ap[:, g, :n], bank.ap[:, :n], AF.Sigmoid, [bank.dep], [ev.dep])
                else:
                    self.cp("dve", ev.ap[:, g, :n], bank.ap[:, :n], [bank.dep], [ev.dep])
                if g == 1:
                    P.dma("sp", self.zs[b, cb - 1:cb + 1].rearrange("c p t -> p c t")[:, :, t0:t0 + n], ev.ap[:, :, :n], reads=[ev.dep])
                evi += 1
        P.barrier()
        A.reset(mark)

    def phase_rg(self, l):
        P, A = self.P, self.A
        mark = A.mark()
        ZW = NT + 6
        zx = [A.alloc([ZW], BF16) for _ in range(2)]
        zg = [A.alloc([NT], BF16) for _ in range(2)]
        for z in zx:
            self.memset("pool", z.ap, 0.0, [z.dep])
        xc = A.alloc([NT], BF16)
        Rb = A.alloc([NT], F32)
        Ib = A.alloc([NT], F32)
        Tb = A.alloc([NT], F32)
        B1 = A.alloc([NT], F32)
        yb = A.alloc([NT], BF16)
        dg = [A.alloc([4, 128], BF16) for _ in range(2)]
        gw = [A.alloc([4, 128], BF16) for _ in range(2)]
        for g_ in gw:
            self.memset("pool", g_.ap, 0.0, [g_.dep])

        def load(c, b, slot):
            P.dma("sp", zx[slot].ap[:, 1:1 + CTX], self.zs[b, CB_RGX + c, :, 0:CTX], writes=[zx[slot].dep])
            P.dma("sp", zx[slot].ap[:, 4 + CTX:4 + NT], self.zs[b, CB_RGX + c, :, CTX:NT], writes=[zx[slot].dep])
            P.dma("sp", zg[slot].ap, self.zs[b, CB_RGG + c], writes=[zg[slot].dep])
        seq = [(c, b) for c in range(KC) for b in range(self.nb)]
        load(seq[0][0], seq[0][1], 0)
        for si, (c, b) in enumerate(seq):
            slot = si % 2
            if b == 0:
                ws = c % 2
                for j in range(4):
                    self.ts("dve", dg[ws].ap[:, j], self.ident.ap, self.vcol(f"rg_cw{l}_{j}", c), None, ALU.mult, None,
                            [self.ident.dep, self.vecs.dep], [dg[ws].dep])
                for d in range(2):
                    for gi, src_w in ((0, self.rg_wa), (1, self.rg_wx)):
                        for hh in range(2):
                            P.dma("pool", gw[ws].ap[64 * hh:64 * hh + 64, 2 * d + gi, 64 * hh:64 * hh + 64], src_w[l, d, 2 * c + hh], writes=[gw[ws].dep])
            ws = c % 2
            if si + 1 < len(seq):
                load(seq[si + 1][0], seq[si + 1][1], 1 - slot)
            z = zx[slot]
            for (t0, n) in TILES512:
                base = t0 if t0 < CTX else t0 + 3
                bank = self.next_bank(0, 4)
                for j in range(4):
                    self.mm(bank.ap[:, :n], dg[ws].ap[:, j], z.ap[:, base + j:base + j + n], j == 0, j == 3, [dg[ws].dep, z.dep], [bank.dep], signal=(j == 3))
                self.act(xc.ap[:, t0:t0 + n], bank.ap[:, :n], AF.Identity, [bank.dep, self.vecs.dep], [xc.dep], bias=self.vcol(f"rg_cb{l}", c))
            for d in range(2):
                Tt = Tb if d == 0 else B1
                for (t0, n) in TILES512:
                    for gi, dst, bn in ((0, Rb, f"rg_ba{l}_{d}"), (1, Ib, f"rg_bx{l}_{d}")):
                        bank = self.next_bank(0, 4)
                        self.mm(bank.ap[:, :n], gw[ws].ap[:, 2 * d + gi], xc.ap[:, t0:t0 + n], True, True, [gw[ws].dep, xc.dep], [bank.dep])
                        self.act(dst.ap[:, t0:t0 + n], bank.ap[:, :n], AF.Sigmoid, [bank.dep, self.vecs.dep], [dst.dep], bias=self.vcol(bn, c))
                clc = self.cl.ap[:, l * 2 + d, c:c + 1]
                cl2c = self.cl2.ap[:, l * 2 + d, c:c + 1]
                self.act(Tt.ap, Rb.ap, AF.Exp, [Rb.dep, self.cl2.dep], [Tt.dep], scale=cl2c)
                self.act(Rb.ap, Rb.ap, AF.Exp, [Rb.dep, self.cl.dep], [Rb.dep], scale=clc)
                self.act(Tt.ap, Tt.ap, AF.Sqrt, [Tt.dep, self.cst.dep], [Tt.dep], scale=-1.0, bias=self.cst.ap[:, 1:2])
                self.tt("dve", Ib.ap, Ib.ap, Tt.ap, ALU.mult, [Ib.dep, Tt.dep], [Ib.dep])
                self.tt("dve", Ib.ap, Ib.ap, xc.ap, ALU.mult, [Ib.dep, xc.dep], [Ib.dep])
                if d == 0:
                    oa, a_, b_ = Tt.ap, Rb.ap, Ib.ap
                    P.op("dve", lambda e, oa=oa, a_=a_, b_=b_: e.tensor_tensor_scan(out=oa, data0=a_, data1=b_, initial=0.0, op0=ALU.mult, op1=ALU.add),
                         [Rb.dep, Ib.dep], [Tt.dep])
                else:
                    oa, a_, b_ = Tt.ap[:, 0:CTX][:, ::-1], Rb.ap[:, 0:CTX][:, ::-1], Ib.ap[:, 0:CTX][:, ::-1]
                    P.op("dve", lambda e, oa=oa, a_=a_, b_=b_: e.tensor_tensor_scan(out=oa, data0=a_, data1=b_, initial=0.0, op0=ALU.mult, op1=ALU.add),
                         [Rb.dep, Ib.dep], [Tt.dep])
                    oa, a_, b_ = Tt.ap[:, CTX:NT][:, ::-1], Rb.ap[:, CTX:NT][:, ::-1], Ib.ap[:, CTX:NT][:, ::-1]
                    ini = Tt.ap[:, 0:1]
                    P.op("dve", lambda e, oa=oa, a_=a_, b_=b_, ini=ini: e.tensor_tensor_scan(out=oa, data0=a_, data1=b_, initial=ini, op0=ALU.mult, op1=ALU.add),
                         [Rb.dep, Ib.dep, Tt.dep], [Tt.dep])
            if "hf" in self.dbg_out and b == 0:
                P.dma("sp", self.dbg_out["hf"][c], Tb.ap, reads=[Tb.dep])
            self.tt("dve", Tb.ap, Tb.ap, B1.ap, ALU.add, [Tb.dep, B1.dep], [Tb.dep])
            g = zg[slot]
            self.act(g.ap, g.ap, AF.Gelu_apprx_tanh, [g.dep], [g.dep])
            self.tt("dve", yb.ap, g.ap, Tb.ap, ALU.mult, [g.dep, Tb.dep], [yb.dep])
            P.dma("sp", self.yrg[b, c], yb.ap, reads=[yb.dep])
        P.barrier()
        A.reset(mark)

    def phase_hg(self, l):
        P, A = self.P, self.A
        mark = A.mark()
        HC = 32
        NCH = NT // HC
        wins = [(0, 3), (3, 3), (6, 2)] + [(8 + 3 * i, 3) for i in range(42)] + [(134, 2)]
        NW = len(wins)
        ogroups = [[0, 1, 2]] + [list(range(3 + 5 * g, 3 + 5 * g + 5)) for g in range(8)] + [[43, 44, 45]]
        cm_ = A.alloc([NT + 64], BF16)
        self.memset("pool", cm_.ap, 1.0, [cm_.dep])
        self.memset("pool", cm_.ap[:, 0:NT + 1:HC], 0.0, [cm_.dep])
        cmf = Tl(cm_.ap[:, 0:NT], cm_.dep)
        cmb = Tl(cm_.ap[:, 1:NT + 1], cm_.dep)
        zq = A.alloc([NT], BF16)
        zsg = [A.alloc([NT], BF16) for _ in range(2)]
        zv = A.alloc([NT], BF16)
        og = A.alloc([NT], BF16)
        Lb = A.alloc([NT], F32)
        Eb = A.alloc([NT], BF16)
        Kb = A.alloc([NT], BF16)
        qe = [A.alloc([NT], BF16) for _ in range(2)]
        keT = [A.alloc([NW, 128], BF16) for _ in range(2)]
        AT = [A.alloc([NW, 96], BF16) for _ in range(2)]
        ebs = [A.alloc([NCH], F32) for _ in range(2)]
        vso = Eb
        vTx = A.alloc([NW, 384], BF16)
        oacc = Lb
        Sb = [[A.alloc([128], BF16) for _ in range(2)] for _ in range(2)]
        sq = [A.alloc([512], BF16) for _ in range(2)]
        rsb = [A.alloc([512], F32) for _ in range(2)]
        yb = Kb
        obank = [self.banks[0], self.banks[1]]
        tbanks = [self.banks[2], self.banks[3]]
        sgb = [[self.banks[4], self.banks[5]], [self.banks[6], self.banks[7]]]
        ssbank = self.banks[2]

        def so3(ap_rm):
            return ap_rm[:, CTX:NT].rearrange("p (r c) -> p c r", c=GW)

        def nat3(ap_so):
            return ap_so[:, CTX:NT].rearrange("p (c r) -> p c r", r=GW)

        def load(hd, b):
            for dst, cb in ((zq, CB_Q), (zsg[0], CB_FF), (zsg[1], CB_FB), (zv, CB_V)):
                P.dma("sp", dst.ap, self.zs[b, cb + hd], writes=[dst.dep])
        seq = [(hd, b) for hd in range(KC) for b in range(self.nb)]
        load(seq[0][0], seq[0][1])
        tri = 0
        for si, (hd, b) in enumerate(seq):
            P.dma("sp", og.ap, self.zs[b, CB_OG + hd], writes=[og.dep])
            lbc = self.lbv.ap[:, l, hd:hd + 1]
            omlc = self.oml.ap[:, l, hd:hd + 1]
            nomlc = self.noml.ap[:, l, hd:hd + 1]
            self.cp("pool", vso.ap[:, 0:CTX], zv.ap[:, 0:CTX], [zv.dep], [vso.dep])
            self.cp("pool", nat3(vso.ap), so3(zv.ap), [zv.dep], [vso.dep])

            def transposes(src, dst):
                nonlocal tri
                for j0 in range(0, NW, 8):
                    nj = min(8, NW - j0)
                    tb = tbanks[tri % 2]
                    tri += 1
                    tbv = tb.ap.bitcast(BF16)
                    for jj in range(nj):
                        c0, ncw = wins[j0 + jj]
                        nt_ = ncw * HC
                        self.tr(tbv[:nt_, jj * 128:(jj + 1) * 128], src.ap[:, c0 * HC:c0 * HC + nt_], [src.dep], [tb.dep], signal=(jj == nj - 1))
                    if tri % 2:
                        self.cp("dve", dst.ap[:96, j0:j0 + nj, :], tbv[:96, 0:nj * 128].rearrange("p (j c) -> p j c", c=128), [tb.dep], [dst.dep])
                    else:
                        self.act(dst.ap[:96, j0:j0 + nj, :], tbv[:96, 0:nj * 128].rearrange("p (j c) -> p j c", c=128), AF.Copy, [tb.dep], [dst.dep])
            for j0 in range(0, NW, 8):
                nj = min(8, NW - j0)
                tb = tbanks[tri % 2]
                tri += 1
                tbv = tb.ap.bitcast(BF16)
                for jj in range(nj):
                    c0, ncw = wins[j0 + jj]
                    nt_ = ncw * HC
                    self.tr(tbv[:nt_, jj * 128:(jj + 1) * 128], vso.ap[:, c0 * HC:c0 * HC + nt_], [vso.dep], [tb.dep], signal=(jj == nj - 1))
                for h in range(3):
                    self.ts("dve", vTx.ap[:96, j0:j0 + nj, 128 * h:128 * h + 128], tbv[:96, 0:nj * 128].rearrange("p (j c) -> p j c", c=128),
                            self.cst.ap[:96, 2 + h:3 + h], None, ALU.mult, None, [tb.dep, self.cst.dep], [vTx.dep])
            for d in range(2):
                sg = zsg[d]
                cm = cmf if d == 0 else cmb
                self.act(Lb.ap[:, 0:CTX], sg.ap[:, 0:CTX], AF.Ln, [sg.dep, self.oml.dep, self.lbv.dep], [Lb.dep], scale=omlc, bias=lbc)
                self.act(nat3(Lb.ap), so3(sg.ap), AF.Ln, [sg.dep, self.oml.dep, self.lbv.dep], [Lb.dep], scale=omlc, bias=lbc)
                self.ts("dve", Kb.ap[:, 0:CTX], sg.ap[:, 0:CTX], nomlc, omlc, ALU.mult, ALU.add, [sg.dep, self.oml.dep, self.noml.dep], [Kb.dep])
                self.ts("dve", nat3(Kb.ap), so3(sg.ap), nomlc, omlc, ALU.mult, ALU.add, [sg.dep, self.oml.dep, self.noml.dep], [Kb.dep])
                if d == 0:
                    oa, m_, l_ = Lb.ap, cm.ap, Lb.ap
                else:
                    oa, m_, l_ = Lb.ap[:, ::-1], cm.ap[:, ::-1], Lb.ap[:, ::-1]
                P.op("dve", lambda e, oa=oa, m_=m_, l_=l_: e.tensor_tensor_scan(out=oa, data0=m_, data1=l_, initial=0.0, op0=ALU.mult, op1=ALU.add),
                     [cm.dep, Lb.dep], [Lb.dep])
                ends = Lb.ap[:, HC - 1:NT:HC] if d == 0 else Lb.ap[:, 0:NT:HC]
                self.act(ebs[d].ap, ends, AF.Exp, [Lb.dep], [ebs[d].dep])
                self.act(Eb.ap, Lb.ap, AF.Exp, [Lb.dep], [Eb.dep])
                self.tt("dve", qe[d].ap[:, 0:CTX], zq.ap[:, 0:CTX], Eb.ap[:, 0:CTX], ALU.mult, [zq.dep, Eb.dep], [qe[d].dep])
                self.tt("dve", nat3(qe[d].ap), so3(zq.ap), nat3(Eb.ap), ALU.mult, [zq.dep, Eb.dep], [qe[d].dep])
                self.act(Eb.ap, Lb.ap, AF.Exp, [Lb.dep], [Eb.dep], scale=-1.0)
                self.tt("dve", Kb.ap, Kb.ap, Eb.ap, ALU.mult, [Kb.dep, Eb.dep], [Kb.dep])
                transposes(Kb, keT[d])
                msk = self.m2f if d == 0 else self.m2b
                for j0 in range(0, NW, 5):
                    nj = min(5, NW - j0)
                    tb = tbanks[tri % 2]
                    tri += 1
                    for jj in range(nj):
                        c0, ncw = wins[j0 + jj]
                        nt_ = ncw * HC
                        tk = slice(c0 * HC, c0 * HC + nt_)
                        self.mm(tb.ap[:nt_, jj * 96:jj * 96 + nt_], Kb.ap[:, tk], qe[d].ap[:, tk], True, True,
                                [Kb.dep, qe[d].dep], [tb.dep], signal=(jj == nj - 1))
                    self.tt("dve", AT[d].ap[:96, j0:j0 + nj, :], tb.ap[:96, 0:nj * 96].rearrange("p (j c) -> p j c", c=96),
                            msk.ap[:96, 0:96].unsqueeze(1).broadcast_to([96, nj, 96]), ALU.mult, [tb.dep, msk.dep], [AT[d].dep])
            if si + 1 < len(seq):
                load(seq[si + 1][0], seq[si + 1][1])
            for d in range(2):
                self.memset("pool", Sb[d][0].ap, 0.0, [Sb[d][0].dep])
            scur = [0, 0]
            wcount = [0, 0]
            fw = [(gi, w) for gi, g in enumerate(ogroups) for w in g]
            bw = [(0, 2), (0, 1), (0, 0)] + [(gi, w) for gi in range(len(ogroups) - 1, 0, -1) for w in reversed(ogroups[gi])]
            steps = []
            for a_, b_ in zip(fw, bw):
                steps.append((0, a_))
                steps.append((1, b_))
            gcount = [[0] * len(ogroups), [0] * len(ogroups)]
            oinit = [False] * len(ogroups)
            for d, (gi, w) in steps:
                g = ogroups[gi]
                c0, ncw = wins[w]
                nt_ = ncw * HC
                ob = obank[d]
                col = (wins[w][0] - wins[g[0]][0]) * HC
                gcount[d][gi] += 1
                last_in_group = gcount[d][gi] == len(g)
                Sg = sgb[d][wcount[d] % 2]
                wcount[d] += 1
                self.mm(Sg.ap[:, 0:128 * ncw], keT[d].ap[:nt_, w, :], vTx.ap[:nt_, w, 0:128 * ncw], True, False, [keT[d].dep, vTx.dep], [Sg.dep], signal=False)
                hs = list(range(ncw)) if d == 0 else list(range(ncw - 1, -1, -1))
                for hi_, h in enumerate(hs):
                    ch = c0 + h
                    S = Sb[d][scur[d]]
                    Sn = Sb[d][1 - scur[d]]
                    lastq = hi_ == len(hs) - 1
                    oreg = ob.ap[:, col + HC * h:col + HC * h + HC]
                    self.mm(oreg, vTx.ap[:nt_, w, 128 * h:128 * h + 128], AT[d].ap[:nt_, w, HC * h:HC * h + HC], True, False, [vTx.dep, AT[d].dep], [ob.dep], signal=False)
                    self.mm(oreg, S.ap, qe[d].ap[:, ch * HC:ch * HC + HC], False, True, [S.dep, qe[d].dep], [ob.dep], signal=False)
                    self.mm(Sg.ap[:, 128 * h:128 * h + 128], self.ident.ap, S.ap, False, lastq, [self.ident.dep, S.dep], [Sg.dep])
                    self.act(Sn.ap, Sg.ap[:, 128 * h:128 * h + 128], AF.Identity, [Sg.dep, ebs[d].dep], [Sn.dep], scale=ebs[d].ap[:, ch:ch + 1])
                    scur[d] = 1 - scur[d]
                if last_in_group:
                    t0 = wins[g[0]][0] * HC
                    n = sum(wins[x][1] for x in g) * HC
                    if not oinit[gi]:
                        oinit[gi] = True
                        self.act(oacc.ap[:, t0:t0 + n], ob.ap[:, :n], AF.Identity, [ob.dep], [oacc.dep])
                    else:
                        self.tt("dve", oacc.ap[:, t0:t0 + n], oacc.ap[:, t0:t0 + n], ob.ap[:, :n], ALU.add, [ob.dep, oacc.dep], [oacc.dep])
            if "ohg" in self.dbg_out and b == 0:
                P.dma("sp", self.dbg_out["ohg"][hd], oacc.ap, reads=[oacc.dep])
            for gi, (t0, n) in enumerate(TILES512):
                s_ = sq[gi % 2]
                rs_ = rsb[gi % 2]
                self.act(s_.ap[:, :n], oacc.ap[:, t0:t0 + n], AF.Square, [oacc.dep], [s_.dep])
                self.mm(ssbank.ap[:, :n], self.ones.ap, s_.ap[:, :n], True, True, [self.ones.dep, s_.dep], [ssbank.dep])
                self.act(rs_.ap[:, :n], ssbank.ap[:, :n], AF.Ln, [ssbank.dep, self.cst.dep], [rs_.dep], scale=1.0 / 128, bias=self.cst.ap[:, 0:1])
                self.act(rs_.ap[:, :n], rs_.ap[:, :n], AF.Exp, [rs_.dep], [rs_.dep], scale=-0.5)
                self.tt("dve", oacc.ap[:, t0:t0 + n], oacc.ap[:, t0:t0 + n], rs_.ap[:, :n], ALU.mult, [oacc.dep, rs_.dep], [oacc.dep])
            gn = self.vcol(f"hg_gain{l}", hd)
            self.stt(yb.ap[:, 0:CTX], oacc.ap[:, 0:CTX], gn, og.ap[:, 0:CTX], ALU.mult, ALU.mult, [oacc.dep, og.dep, self.vecs.dep], [yb.dep])
            self.stt(yb.ap[:, CTX:NT].rearrange("p (r c) -> p r c", c=GW), oacc.ap[:, CTX:NT].rearrange("p (c r) -> p r c", r=GW), gn,
                     og.ap[:, CTX:NT].rearrange("p (r c) -> p r c", c=GW), ALU.mult, ALU.mult, [oacc.dep, og.dep, self.vecs.dep], [yb.dep])
            P.dma("sp", self.yhg[b, hd], yb.ap, reads=[yb.dep])
        P.barrier()
        A.reset(mark)

    def phase_merge(self, l, src, last):
        P, A = self.P, self.A
        mark = A.mark()
        Ws = []
        for wd in (self.w_prg, self.w_phg, self.w_out):
            W = A.alloc([KC, D], BF16)
            for kc in range(KC):
                P.dma("pool", W.ap[:, kc], wd[l, :, kc], writes=[W.dep])
            Ws.append(W)
        Wr, Wh, Wo = Ws
        ins = [[A.alloc([KC, 512], BF16) for _ in range(4)] for _ in range(2)]
        xts = [A.alloc([KC, 512], F32) for _ in range(2)]
        mb = A.alloc([KC, 512], BF16)
        ysb = A.alloc([KC, 512], F32)
        sq = [A.alloc([512], BF16) for _ in range(2)]
        t1 = [A.alloc([512], F32) for _ in range(2)]
        t2 = [A.alloc([512], F32) for _ in range(2)]
        rt = A.alloc([512], F32)
        rstd = A.alloc([512], F32)
        ssb = self.banks[7]
        tiles = self.tiles(TILES512, skip_ctx=last)

        def load(i):
            b, t0, n, w = tiles[i]
            s = i % 2
            srcs = (self.yrg[b], self.yhg[b], self.zs[b, CB_GA:CB_GA + 8], self.zs[b, CB_GB:CB_GB + 8])
            for dst, sr in zip(ins[s], srcs):
                P.dma("sp", dst.ap[:, :, :n], sr.rearrange("k p t -> p k t")[:, :, t0:t0 + n], writes=[dst.dep])
            P.dma("sp", xts[s].ap[:, :, :n], src[b].rearrange("k p t -> p k t")[:, :, t0:t0 + n], writes=[xts[s].dep])
        load(0)
        for i, (b, t0, n, w) in enumerate(tiles):
            if i + 1 < len(tiles):
                load(i + 1)
            yr, yh, ga, gb = ins[i % 2]
            xt = xts[i % 2]
            for oc in range(KC):
                ba = self.next_bank(0, 6)
                for kc in range(KC):
                    self.mm(ba.ap[:, :n], Wr.ap[:, kc, oc * 128:(oc + 1) * 128], yr.ap[:, kc, :n], kc == 0, kc == KC - 1, [Wr.dep, yr.dep], [ba.dep], signal=(kc == KC - 1))
                bb = self.next_bank(0, 6)
                for kc in range(KC):
                    self.mm(bb.ap[:, :n], Wh.ap[:, kc, oc * 128:(oc + 1) * 128], yh.ap[:, kc, :n], kc == 0, kc == KC - 1, [Wh.dep, yh.dep], [bb.dep], signal=(kc == KC - 1))
                a1, a2 = t1[oc % 2], t2[oc % 2]
                self.tt("dve", a1.ap[:, :n], ba.ap[:, :n], ga.ap[:, oc, :n], ALU.mult, [ba.dep, ga.dep], [a1.dep])
                self.tt("dve", a2.ap[:, :n], bb.ap[:, :n], gb.ap[:, oc, :n], ALU.mult, [bb.dep, gb.dep], [a2.dep])
                self.tt("pool", mb.ap[:, oc, :n], a1.ap[:, :n], a2.ap[:, :n], ALU.add, [a1.dep, a2.dep], [mb.dep])
            for oc in range(KC):
                by = self.next_bank(0, 6)
                for kc in range(KC):
                    self.mm(by.ap[:, :n], Wo.ap[:, kc, oc * 128:(oc + 1) * 128], mb.ap[:, kc, :n], kc == 0, kc == KC - 1, [Wo.dep, mb.dep], [by.dep], signal=(kc == KC - 1))
                self.act(ysb.ap[:, oc, :n], by.ap[:, :n], AF.Identity, [by.dep], [ysb.dep])
                s = sq[oc % 2]
                self.act(s.ap[:, :n], by.ap[:, :n], AF.Square, [by.dep], [s.dep])
                self.mm(ssb.ap[:, :n], self.ones.ap, s.ap[:, :n], oc == 0, oc == KC - 1, [self.ones.dep, s.dep], [ssb.dep])
            self.rstd_from_ss(ssb, n, 1.0 / D, rt, rstd)
            for oc in range(KC):
                a1 = t1[oc % 2]
                self.stt(a1.ap[:, :n], ysb.ap[:, oc, :n], self.gt1.ap[:, l, oc, w:w + 1], rstd.ap[:, :n], ALU.mult, ALU.mult,
                         [ysb.dep, self.gt1.dep, rstd.dep], [a1.dep])
                self.tt("pool", xt.ap[:, oc, :n], xt.ap[:, oc, :n], a1.ap[:, :n], ALU.add, [xt.dep, a1.dep], [xt.dep])
            P.dma("sp", self.xs[b].rearrange("k p t -> p k t")[:, :, t0:t0 + n], xt.ap[:, :, :n], reads=[xt.dep])
        P.barrier()
        A.reset(mark)

    def phase_ffn(self, l, last):
        P, A = self.P, self.A
        mark = A.mark()
        N = 256
        Wu = A.alloc([KC, 2 * DFF], BF16)
        for kc in range(KC):
            for q in range(4):
                P.dma("pool", Wu.ap[:, kc, q * 1408:(q + 1) * 1408], self.w_up[l, :, kc, q * 1408:(q + 1) * 1408], writes=[Wu.dep])
        Wd = A.alloc([FFC, D], BF16)
        for fc in range(FFC):
            P.dma("pool", Wd.ap[:, fc], self.w_dn[l, :, fc], writes=[Wd.dep])
        dgr = [A.alloc([3, 128], BF16) for _ in range(4)]
        xts = [A.alloc([KC, N], F32) for _ in range(2)]
        hbs = [A.alloc([KC, N], BF16) for _ in range(2)]
        sq = [A.alloc([N], BF16) for _ in range(2)]
        rt = A.alloc([N], F32)
        rstd = A.alloc([N], F32)
        rt2 = A.alloc([N], F32)
        rstd2 = A.alloc([N], F32)
        tmps = [A.alloc([N], F32) for _ in range(2)]
        gp = [A.alloc([4, GW + 2], BF16) for _ in range(2)]
        gpc = [A.alloc([N + 2], BF16) for _ in range(2)]
        for g_ in gp + gpc:
            self.memset("pool", g_.ap, 0.0, [g_.dep])
        sb = [A.alloc([N], BF16) for _ in range(2)]
        ab = A.alloc([FFC, N], BF16)
        ysb = A.alloc([KC, N], F32)
        ssb = self.banks[7]
        tiles = self.tiles(TILES256, skip_ctx=last)

        def norm(i):
            b, t0, n, w = tiles[i]
            self.norm_tile(self.xs, b, t0, n, w, l, self.sc2, 3, xts[i % 2], hbs[i % 2], sq, rt, rstd, tmps, ssb)
        norm(0)
        for i, (b, t0, n, w) in enumerate(tiles):
            hb = hbs[i % 2]
            xr = xts[i % 2]
            isctx = t0 < CTX
            for fc in range(FFC):
                dgs = dgr[fc % 4]
                for j in range(3):
                    self.ts("dve", dgs.ap[:, j], self.ident.ap, self.vcol(f"ffn_cw{l}_{j}", fc), None, ALU.mult, None,
                            [self.ident.dep, self.vecs.dep], [dgs.dep])
                bg = self.next_bank(0, 6)
                for kc in range(KC):
                    self.mm(bg.ap[:, :N], Wu.ap[:, kc, fc * 128:(fc + 1) * 128], hb.ap[:, kc, :], kc == 0, kc == KC - 1, [Wu.dep, hb.dep], [bg.dep], signal=(kc == KC - 1))
                bv = self.next_bank(0, 6)
                for kc in range(KC):
                    self.mm(bv.ap[:, :N], Wu.ap[:, kc, DFF + fc * 128:DFF + (fc + 1) * 128], hb.ap[:, kc, :], kc == 0, kc == KC - 1, [Wu.dep, hb.dep], [bv.dep], signal=(kc == KC - 1))
                bc = self.next_bank(0, 6)
                if isctx:
                    g_ = gpc[fc % 2]
                    self.act(g_.ap[:, 1:N + 1], bg.ap[:, :N], AF.Identity, [bg.dep], [g_.dep])
                    for j in range(3):
                        self.mm(bc.ap[:, :N], dgs.ap[:, j], g_.ap[:, j:j + N], j == 0, j == 2, [dgs.dep, g_.dep], [bc.dep], signal=(j == 2))
                else:
                    g_ = gp[fc % 2]
                    self.act(g_.ap[:, :, 1:GW + 1], bg.ap[:, :N].rearrange("p (r c) -> p r c", c=GW), AF.Identity, [bg.dep], [g_.dep])
                    for j in range(3):
                        self.mm(bc.ap[:, :N].rearrange("p (r c) -> p r c", c=GW), dgs.ap[:, j], g_.ap[:, :, j:j + GW], j == 0, j == 2, [dgs.dep, g_.dep], [bc.dep], signal=(j == 2))
                s = sb[fc % 2]
                self.act(s.ap, bc.ap[:, :N], AF.Silu, [bc.dep, self.vecs.dep], [s.dep], bias=self.vcol(f"ffn_cb{l}", fc))
                self.tt("dve", ab.ap[:, fc, :], s.ap, bv.ap[:, :N], ALU.mult, [s.dep, bv.dep], [ab.dep])
            if i + 1 < len(tiles):
                norm(i + 1)
            for oc in range(KC):
                by = self.next_bank(0, 6)
                for fc in range(FFC):
                    self.mm(by.ap[:, :N], Wd.ap[:, fc, oc * 128:(oc + 1) * 128], ab.ap[:, fc, :], fc == 0, fc == FFC - 1, [Wd.dep, ab.dep], [by.dep], signal=(fc == FFC - 1))
                self.act(ysb.ap[:, oc, :], by.ap[:, :N], AF.Identity, [by.dep], [ysb.dep])
                s = sq[oc % 2]
                self.act(s.ap[:, :N], by.ap[:, :N], AF.Square, [by.dep], [s.dep])
                self.mm(self.banks[6].ap[:, :N], self.ones.ap, s.ap[:, :N], oc == 0, oc == KC - 1, [self.ones.dep, s.dep], [self.banks[6].dep])
            self.rstd_from_ss(self.banks[6], N, 1.0 / D, rt2, rstd2)
            for oc in range(KC):
                self.stt(ysb.ap[:, oc, :], ysb.ap[:, oc, :], self.gt2.ap[:, l, oc, w:w + 1], rstd2.ap[:, :N], ALU.mult, ALU.mult,
                         [ysb.dep, self.gt2.dep, rstd2.dep], [ysb.dep])
                self.tt("pool", xr.ap[:, oc, :], xr.ap[:, oc, :], ysb.ap[:, oc, :], ALU.add, [xr.dep, ysb.dep], [xr.dep])
            if last:
                P.dma("sp", self.out[b].rearrange("k p t -> p k t")[:, :, t0 - CTX:t0 - CTX + n], xr.ap, reads=[xr.dep])
            else:
                P.dma("sp", self.xs[b].rearrange("k p t -> p k t")[:, :, t0:t0 + n], xr.ap, reads=[xr.dep])
        P.barrier()
        A.reset(mark)

    def build(self):
        nc = self.nc
        for name in self.dbg:
            if name == "hf":
                self.dbg_out["hf"] = nc.dram_tensor("hf", [KC, 128, NT], F32, kind="ExternalOutput").ap()
            if name == "ohg":
                self.dbg_out["ohg"] = nc.dram_tensor("ohg", [KC, 128, NT], F32, kind="ExternalOutput").ap()
            if name == "mod":
                self.dbg_out["mod"] = nc.dram_tensor("mod", [128, DEPTH * 6 * KC * 3], F32, kind="ExternalOutput").ap()
                self.dbg_out["cl"] = nc.dram_tensor("cl", [128, DEPTH * 2 * KC], F32, kind="ExternalOutput").ap()
                self.dbg_out["lb"] = nc.dram_tensor("lb", [128, DEPTH * KC], F32, kind="ExternalOutput").ap()
        self.phase0()
        if "mod" in self.dbg_out:
            self.P.dma("sp", self.dbg_out["mod"], self.modraw.ap.rearrange("p a k w -> p (a k w)"), reads=[self.modraw.dep])
            self.P.dma("sp", self.dbg_out["cl"], self.cl.ap.rearrange("p a k -> p (a k)"), reads=[self.cl.dep])
            self.P.dma("sp", self.dbg_out["lb"], self.lbv.ap.rearrange("p a k -> p (a k)"), reads=[self.lbv.dep])
        for l in self.layers:
            last = (l == DEPTH - 1)
            src = self.xin if l == 0 else self.xs
            if "P" in self.phases:
                self.phase_proj(l, src)
            if "R" in self.phases:
                self.phase_rg(l)
            if "H" in self.phases:
                self.phase_hg(l)
            if "C" in self.phases:
                self.phase_merge(l, src, last)
            if "D" in self.phases:
                self.phase_ffn(l, last)
        self.P.barrier()
        self.P.finish()
        return nc


def fm(v, ncol):
    return np.ascontiguousarray(np.asarray(v, np.float32).reshape(ncol, 128).T)


def pack_vecs(inp, core, nb=NB):
    V = np.zeros((128, NV), np.float32)

    def put(name, arr):
        o, n = VOFF[name]
        V[:, o:o + n] = arr
    for l in range(DEPTH):
        for n in ("g_pre_mix", "g_post_mix", "g_pre_ffn", "g_post_ffn"):
            put(f"{n}{l}", fm(inp[n][l], 8))
        for j in range(4):
            put(f"rg_cw{l}_{j}", fm(inp["rg_conv_w"][l, j], 8))
        put(f"rg_cb{l}", fm(inp["rg_conv_b"][l], 8))
        for d in range(2):
            put(f"rg_ba{l}_{d}", fm(inp["rg_ba"][l, d], 8))
            put(f"rg_bx{l}_{d}", fm(inp["rg_bx"][l, d], 8))
        put(f"hg_gain{l}", fm(inp["hg_out_norm"][l], 8))
        for j in range(3):
            put(f"ffn_cw{l}_{j}", fm(inp["ffn_conv_w"][l, j], FFC))
        put(f"ffn_cb{l}", fm(inp["ffn_conv_b"][l], FFC))
        put(f"b_ada{l}", fm(inp["b_ada"][l], 48))
    lam = np.concatenate([fm(inp["rg_lam"][l, d], 8) for l in range(DEPTH) for d in range(2)], axis=1)
    put("rg_lam", lam)
    put("hg_lb", np.concatenate([fm(inp["hg_lb_logits"][l], 8) for l in range(DEPTH)], axis=1))
    c3 = np.zeros((128, 8, 3), np.float32)
    for w in range(nb):
        c3[:, :, w] = fm(inp["c"][core * nb + w], 8)
    c3[:, :, 2] = fm(inp["c_ctx"], 8)
    put("c3", c3.reshape(128, 24))
    return V


_SHARED = {}


def shared_weights(inp):
    key = id(inp["w_in"])
    if _SHARED.get("key") == key:
        return _SHARED["val"]
    f = lambda a: np.ascontiguousarray(np.asarray(a, np.float32))
    w = {}
    w["w_ada"] = f(np.asarray(inp["w_ada"]).reshape(DEPTH, KC, 128, 6, D).transpose(0, 3, 2, 1, 4))
    w["w_in"] = f(np.asarray(inp["w_in"]).reshape(DEPTH, KC, 128, NCB, 128).transpose(0, 3, 2, 1, 4)).reshape(DEPTH, NCB, 128, KC * 128)
    w["rg_wa"] = f(inp["rg_wa"])
    w["rg_wx"] = f(inp["rg_wx"])
    for n, k in (("w_prg", "w_proj_rg"), ("w_phg", "w_proj_hg"), ("w_out", "w_out")):
        w[n] = f(np.asarray(inp[k]).reshape(DEPTH, KC, 128, D).transpose(0, 2, 1, 3))
    w["w_up"] = f(np.asarray(inp["ffn_w_up"]).reshape(DEPTH, KC, 128, 2 * DFF).transpose(0, 2, 1, 3))
    w["w_dn"] = f(np.asarray(inp["ffn_w_down"]).reshape(DEPTH, FFC, 128, D).transpose(0, 2, 1, 3))
    _SHARED["key"] = key
    _SHARED["val"] = w
    return w


def core_inputs(inp, core, nb=NB):
    w = shared_weights(inp)
    xin = np.empty((nb, KC, 128, NT), np.float32)
    for i in range(nb):
        bi = core * nb + i
        xin[i, :, :, :CTX] = np.asarray(inp["ctx"][bi]).T.reshape(KC, 128, CTX)
        xin[i, :, :, CTX:] = np.asarray(inp["x"][bi]).T.reshape(KC, 128, LAT)
    m = dict(w)
    m["xin"] = xin
    m["vecs"] = pack_vecs(inp, core, nb)
    return m


_NC_CACHE = {}


def kernel(**inputs):
    inp = {k: np.asarray(v) for k, v in inputs.items()}
    if "nc" not in _NC_CACHE:
        _NC_CACHE["nc"] = Builder().build()
    nc = _NC_CACHE["nc"]
    in_maps = [core_inputs(inp, c) for c in range(NCORES)]
    res = run_bass_kernel_spmd(nc, in_maps, core_ids=list(range(NCORES)))
    out = np.empty((NCORES * NB, LAT, D), np.float32)
    for c in range(NCORES):
        o = res.results[c]["out"]
        for i in range(NB):
            out[c * NB + i] = o[i].reshape(D, LAT).T
    return out
```

```python
import contextlib
import numpy as np
import concourse.bass as bass
import concourse.mybir as mybir
from concourse.bass_utils import run_bass_kernel_spmd

F32 = mybir.dt.float32
BF16 = mybir.dt.bfloat16
AF = mybir.ActivationFunctionType
ALU = mybir.AluOpType

ENGS = ("pe", "act", "dve", "pool", "sp")

NCORES = 8
NB = 2
DEPTH = 4
D = 1024
KC = 8
CTX = 256
LAT = 4096
NT = CTX + LAT
GW = 64
DFF = 2816
FFC = 22
NCB = 72
EPS = 1e-6
ARENA_BYTES = 211968
TILES512 = [(0, 256)] + [(256 + 512 * i, 512) for i in range(8)]
TILES256 = [(256 * i, 256) for i in range(17)]
CB_RGX, CB_RGG, CB_Q, CB_FF, CB_FB, CB_V, CB_OG, CB_GA, CB_GB = 0, 8, 16, 24, 32, 40, 48, 56, 64


def vec_layout():
    ent = []
    for l in range(DEPTH):
        for n in ("g_pre_mix", "g_post_mix", "g_pre_ffn", "g_post_ffn"):
            ent.append((f"{n}{l}", 8))
        for j in range(4):
            ent.append((f"rg_cw{l}_{j}", 8))
        ent.append((f"rg_cb{l}", 8))
        for d in range(2):
            ent.append((f"rg_ba{l}_{d}", 8))
            ent.append((f"rg_bx{l}_{d}", 8))
        ent.append((f"hg_gain{l}", 8))
        for j in range(3):
            ent.append((f"ffn_cw{l}_{j}", FFC))
        ent.append((f"ffn_cb{l}", FFC))
        ent.append((f"b_ada{l}", 48))
    ent.append(("rg_lam", 64))
    ent.append(("hg_lb", 32))
    ent.append(("c3", 24))
    off = {}
    o = 0
    for n, k in ent:
        off[n] = (o, k)
        o += k
    return off, o


VOFF, NV = vec_layout()


class Dep:
    __slots__ = ("w", "r")

    def __init__(self):
        self.w = {}
        self.r = {}


class Tl:
    __slots__ = ("ap", "dep")

    def __init__(self, ap, dep=None):
        self.ap = ap
        self.dep = dep if dep is not None else Dep()


class Prog:
    def __init__(self, nc, n_dma_sems=(("sp", 12), ("pool", 8))):
        self.nc = nc
        self.es = contextlib.ExitStack()
        self.prog = {e: [] for e in ENGS}
        self.sems = {}
        self.cnt = {}
        self.waited = {e: {} for e in ENGS}
        for e in ENGS:
            self.sems[e] = self.es.enter_context(nc.semaphore("c_" + e))
            self.cnt[e] = 0
        self.dma_pool = {}
        for q, n in n_dma_sems:
            lst = []
            for i in range(n):
                key = "d_%s%d" % (q, i)
                self.sems[key] = self.es.enter_context(nc.semaphore(key))
                self.cnt[key] = 0
                lst.append(key)
            self.dma_pool[q] = [lst, 0]
        self.n_inst = 0

    def _waits(self, eng, reads, writes, extra=()):
        need = {}
        for d in reads:
            for k, v in d.w.items():
                if need.get(k, 0) < v:
                    need[k] = v
        for d in writes:
            for k, v in d.w.items():
                if need.get(k, 0) < v:
                    need[k] = v
            for k, v in d.r.items():
                if need.get(k, 0) < v:
                    need[k] = v
        for k, v in extra:
            if need.get(k, 0) < v:
                need[k] = v
        out = []
        wd = self.waited[eng]
        for k, v in need.items():
            if k == eng and eng == "pe":
                continue
            if wd.get(k, 0) >= v:
                continue
            wd[k] = v
            out.append((k, v))
        return out

    def _mark(self, ev, reads, writes):
        k, v = ev
        for d in reads:
            if d.r.get(k, 0) < v:
                d.r[k] = v
        for d in writes:
            if d.w.get(k, 0) < v:
                d.w[k] = v

    def op(self, eng, fn, reads=(), writes=(), signal=True):
        waits = self._waits(eng, reads, writes)
        if signal:
            self.cnt[eng] += 1
            ev = (eng, self.cnt[eng])
        else:
            ev = (eng, self.cnt[eng] + 1)
        sems = self.sems
        semE = sems[eng]

        def emit(e, waits=waits, fn=fn, signal=signal):
            for k, v in waits:
                e.wait_ge(sems[k], v)
            ins = fn(e)
            if signal:
                ins.then_inc(semE, 1)
        self.prog[eng].append(emit)
        self._mark(ev, reads, writes)
        self.n_inst += 1
        return ev

    def dma(self, q, out, in_, reads=(), writes=(), **kw):
        lst, idx = self.dma_pool[q]
        key = lst[idx % len(lst)]
        self.dma_pool[q][1] = idx + 1
        extra = ((key, self.cnt[key]),) if self.cnt[key] else ()
        waits = self._waits(q, reads, writes, extra=extra)
        self.cnt[key] += 16
        ev = (key, self.cnt[key])
        sems = self.sems

        def emit(e, waits=waits, out=out, in_=in_, kw=kw, key=key):
            for k, v in waits:
                e.wait_ge(sems[k], v)
            e.dma_start(out=out, in_=in_, **kw).then_inc(sems[key], 16)
        self.prog[q].append(emit)
        self._mark(ev, reads, writes)
        self.n_inst += 1
        return ev

    def barrier(self):
        need = [(k, v) for k, v in self.cnt.items() if v > 0]
        sems = self.sems
        for eng in ENGS:
            wd = self.waited[eng]
            ws = []
            for k, v in need:
                if wd.get(k, 0) >= v:
                    continue
                wd[k] = v
                ws.append((k, v))

            def emit(e, ws=ws):
                for k, v in ws:
                    e.wait_ge(sems[k], v)
            self.prog[eng].append(emit)

    def finish(self):
        nc = self.nc
        prog = self.prog
        with nc.Block() as block:
            @block.tensor
            def _(e):
                for f in prog["pe"]:
                    f(e)

            @block.scalar
            def _(e):
                for f in prog["act"]:
                    f(e)

            @block.vector
            def _(e):
                for f in prog["dve"]:
                    f(e)

            @block.gpsimd
            def _(e):
                for f in prog["pool"]:
                    f(e)

            @block.sync
            def _(e):
                for f in prog["sp"]:
                    f(e)
        self.es.close()


class Arena:
    def __init__(self, P, nbytes):
        self.P = P
        self.t = P.es.enter_context(P.nc.sbuf_tensor("arena", [128, nbytes // 4], F32))
        self.nbytes = nbytes
        self.off = 0

    def mark(self):
        return self.off

    def reset(self, m):
        self.off = m

    def alloc(self, shape, dt):
        n = 1
        for s in shape:
            n *= s
        esz = 4 if dt == F32 else 2
        nb = (n * esz + 63) // 64 * 64
        assert self.off + nb <= self.nbytes, ("arena overflow", self.off, nb, self.nbytes)
        v = self.t[:, self.off // 4:(self.off + nb) // 4]
        self.off += nb
        if dt != F32:
            v = v.bitcast(dt)
        v = v[:, 0:n]
        if len(shape) == 2:
            v = v.rearrange("p (a b) -> p a b", a=shape[0])
        elif len(shape) == 3:
            v = v.rearrange("p (a b c) -> p a b c", a=shape[0], b=shape[1])
        return Tl(v)


class Builder:
    def __init__(self, nb=NB, layers=range(DEPTH), phases="PRHCD", dbg=()):
        self.nb = nb
        self.layers = list(layers)
        self.phases = phases
        self.dbg = set(dbg)
        nc = bass.Bass("TRN2", target_bir_lowering=False)
        self.nc = nc

        def I(name, shape, dt=F32):
            return nc.dram_tensor(name, list(shape), dt, kind="ExternalInput").ap()

        def S(name, shape, dt):
            kind = "ExternalOutput" if name in self.dbg else "Internal"
            return nc.dram_tensor(name, list(shape), dt, kind=kind).ap()

        self.xin = I("xin", [nb, KC, 128, NT])
        self.vecs_d = I("vecs", [128, NV])
        self.w_ada = I("w_ada", [DEPTH, 6, 128, KC, D])
        self.w_in = I("w_in", [DEPTH, NCB, 128, KC * 128])
        self.rg_wa = I("rg_wa", [DEPTH, 2, 16, 64, 64])
        self.rg_wx = I("rg_wx", [DEPTH, 2, 16, 64, 64])
        self.w_prg = I("w_prg", [DEPTH, 128, KC, D])
        self.w_phg = I("w_phg", [DEPTH, 128, KC, D])
        self.w_out = I("w_out", [DEPTH, 128, KC, D])
        self.w_up = I("w_up", [DEPTH, 128, KC, 2 * DFF])
        self.w_dn = I("w_dn", [DEPTH, 128, FFC, D])
        self.out = nc.dram_tensor("out", [nb, KC, 128, LAT], F32, kind="ExternalOutput").ap()
        self.zs = S("zs", [nb, NCB, 128, NT], BF16)
        self.yrg = S("yrg", [nb, KC, 128, NT], BF16)
        self.yhg = S("yhg", [nb, KC, 128, NT], BF16)
        self.xs = S("xs", [nb, KC, 128, NT], F32)
        self.dbg_out = {}
        self.P = Prog(nc)
        self.A = Arena(self.P, ARENA_BYTES)
        self.banks = [Tl(self.P.es.enter_context(nc.psum_tensor("bank%d" % i, [128, 512], F32))[:]) for i in range(8)]
        self.bank_rr = 0

    def act(self, out, in_, func, reads, writes, scale=1.0, bias=None):
        if bias is None:
            fn = lambda e: e.activation(out=out, in_=in_, func=func, scale=scale)
        else:
            fn = lambda e: e.activation(out=out, in_=in_, func=func, scale=scale, bias=bias)
        return self.P.op("act", fn, reads, writes)

    def tt(self, eng, out, in0, in1, op, reads, writes):
        return self.P.op(eng, lambda e: e.tensor_tensor(out=out, in0=in0, in1=in1, op=op), reads, writes)

    def ts(self, eng, out, in0, s1, s2, op0, op1, reads, writes):
        if s2 is None:
            fn = lambda e: e.tensor_scalar(out=out, in0=in0, scalar1=s1, scalar2=None, op0=op0)
        else:
            fn = lambda e: e.tensor_scalar(out=out, in0=in0, scalar1=s1, scalar2=s2, op0=op0, op1=op1)
        return self.P.op(eng, fn, reads, writes)

    def stt(self, out, in0, scalar, in1, op0, op1, reads, writes):
        return self.P.op("dve", lambda e: e.scalar_tensor_tensor(out=out, in0=in0, scalar=scalar, in1=in1, op0=op0, op1=op1), reads, writes)

    def cp(self, eng, out, in_, reads, writes):
        return self.P.op(eng, lambda e: e.tensor_copy(out=out, in_=in_), reads, writes)

    def mm(self, out, lhsT, rhs, start, stop, reads, writes, signal=True):
        return self.P.op("pe", lambda e: e.matmul(out, lhsT=lhsT, rhs=rhs, start=start, stop=stop), reads, writes, signal=signal)

    def tr(self, out, in_, reads, writes, signal=True):
        ident = self.ident.ap
        return self.P.op("pe", lambda e: e.transpose(out, in_, ident), list(reads) + [self.ident.dep], writes, signal=signal)

    def memset(self, eng, ap, val, writes):
        return self.P.op(eng, lambda e: e.memset(ap, val), (), writes)

    def vcol(self, name, k=None):
        o, n = VOFF[name]
        if k is None:
            return self.vecs.ap[:, o:o + n]
        return self.vecs.ap[:, o + k:o + k + 1]

    def next_bank(self, lo=0, hi=4):
        b = self.banks[lo + self.bank_rr % (hi - lo)]
        self.bank_rr += 1
        return b

    def phase0(self):
        P, A = self.P, self.A
        self.vecs = A.alloc([NV], F32)
        P.dma("sp", self.vecs.ap, self.vecs_d, writes=[self.vecs.dep])
        self.ident = A.alloc([128], BF16)
        self.ones = A.alloc([128], BF16)
        self.m2f = A.alloc([128], BF16)
        self.m2b = A.alloc([128], BF16)
        self.cst = A.alloc([8], F32)
        self.memset("pool", self.cst.ap[:, 0:1], EPS, [self.cst.dep])
        self.memset("pool", self.cst.ap[:, 2:5], 0.0, [self.cst.dep])
        for h in range(3):
            self.memset("pool", self.cst.ap[32 * h:32 * h + 32, 2 + h:3 + h], 1.0, [self.cst.dep])
        self.memset("pool", self.cst.ap[:, 1:2], 1.0, [self.cst.dep])
        self.memset("pool", self.ident.ap, 0.0, [self.ident.dep])
        idap = self.ident.ap
        P.op("pool", lambda e: e.affine_select(out=idap, in_=idap, pattern=[[-1, 128]], compare_op=ALU.not_equal, fill=1.0, base=0, channel_multiplier=1),
             [self.ident.dep], [self.ident.dep])
        self.memset("pool", self.ones.ap, 1.0, [self.ones.dep])
        for m, mult, pat in ((self.m2f, -1, 1), (self.m2b, 1, -1)):
            self.memset("pool", m.ap, 1.0, [m.dep])
            map_ = m.ap
            P.op("pool", lambda e, map_=map_, mult=mult, pat=pat: e.affine_select(out=map_, in_=map_, pattern=[[pat, 128]], compare_op=ALU.is_ge, fill=0.0, base=0, channel_multiplier=mult),
                 [m.dep], [m.dep])
        self.memset("pool", self.m2f.ap[0:32, 32:128], 0.0, [self.m2f.dep])
        self.memset("pool", self.m2f.ap[32:64, 64:128], 0.0, [self.m2f.dep])
        self.memset("pool", self.m2b.ap[32:64, 0:32], 0.0, [self.m2b.dep])
        self.memset("pool", self.m2b.ap[64:128, 0:64], 0.0, [self.m2b.dep])
        self.modraw = A.alloc([DEPTH * 6, KC, 3], F32)
        self.sc1 = A.alloc([DEPTH, KC, 3], F32)
        self.gt1 = A.alloc([DEPTH, KC, 3], F32)
        self.sc2 = A.alloc([DEPTH, KC, 3], F32)
        self.gt2 = A.alloc([DEPTH, KC, 3], F32)
        self.lbv = A.alloc([DEPTH, KC], F32)
        self.oml = A.alloc([DEPTH, KC], F32)
        self.noml = A.alloc([DEPTH, KC], F32)
        self.cl = A.alloc([DEPTH * 2, KC], F32)
        self.cl2 = A.alloc([DEPTH * 2, KC], F32)
        mark = A.mark()
        scv = A.alloc([KC, 3], F32)
        o, n = VOFF["c3"]
        self.act(scv.ap, self.vecs.ap[:, o:o + n].rearrange("p (k w) -> p k w", w=3), AF.Silu, [self.vecs.dep], [scv.dep])
        wbuf = [A.alloc([KC, D], F32) for _ in range(2)]
        it = 0
        for l in range(DEPTH):
            for j in range(6):
                wb = wbuf[it % 2]
                P.dma("sp", wb.ap, self.w_ada[l, j], writes=[wb.dep])
                bank = self.banks[it % 2]
                for oc in range(KC):
                    for kc in range(KC):
                        self.mm(bank.ap[:, oc * 4:oc * 4 + 3], wb.ap[:, kc, oc * 128:(oc + 1) * 128], scv.ap[:, kc, :],
                                kc == 0, kc == KC - 1, [wb.dep, scv.dep], [bank.dep], signal=(kc == KC - 1))
                ob, _ = VOFF[f"b_ada{l}"]
                bia = self.vecs.ap[:, ob + j * 8:ob + j * 8 + 8].unsqueeze(2).broadcast_to([128, KC, 3])
                self.tt("dve", self.modraw.ap[:, l * 6 + j], bank.ap[:, 0:32].rearrange("p (k f) -> p k f", f=4)[:, :, 0:3], bia, ALU.add,
                        [bank.dep, self.vecs.dep], [self.modraw.dep])
                it += 1
        tmp = A.alloc([KC, 3], F32)
        for l in range(DEPTH):
            for (dst, jscale, gname) in ((self.sc1, 1, "g_pre_mix"), (self.sc2, 4, "g_pre_ffn")):
                g = self.vcol(f"{gname}{l}").unsqueeze(2).broadcast_to([128, KC, 3])
                self.ts("dve", tmp.ap, self.modraw.ap[:, l * 6 + jscale], 1.0, None, ALU.add, None, [self.modraw.dep], [tmp.dep])
                self.tt("dve", dst.ap[:, l], tmp.ap, g, ALU.mult, [tmp.dep, self.vecs.dep], [dst.dep])
            for (dst, jg, gname) in ((self.gt1, 2, "g_post_mix"), (self.gt2, 5, "g_post_ffn")):
                g = self.vcol(f"{gname}{l}").unsqueeze(2).broadcast_to([128, KC, 3])
                self.tt("dve", dst.ap[:, l], self.modraw.ap[:, l * 6 + jg], g, ALU.mult, [self.modraw.dep, self.vecs.dep], [dst.dep])
        ex = A.alloc([DEPTH, KC], F32)
        sm = A.alloc([KC], F32)
        self.act(ex.ap, self.vcol("hg_lb").rearrange("p (l k) -> p l k", k=KC), AF.Exp, [self.vecs.dep], [ex.dep])
        self.tt("dve", sm.ap, ex.ap[:, 0], ex.ap[:, 1], ALU.add, [ex.dep], [sm.dep])
        self.tt("dve", sm.ap, sm.ap, ex.ap[:, 2], ALU.add, [ex.dep, sm.dep], [sm.dep])
        self.tt("dve", sm.ap, sm.ap, ex.ap[:, 3], ALU.add, [ex.dep, sm.dep], [sm.dep])
        smap = sm.ap
        P.op("dve", lambda e: e.reciprocal(out=smap, in_=smap), [sm.dep], [sm.dep])
        self.tt("dve", ex.ap, ex.ap, sm.ap.unsqueeze(1).broadcast_to([128, DEPTH, KC]), ALU.mult, [ex.dep, sm.dep], [ex.dep])
        self.memset("dve", self.lbv.ap[:, 0], 0.0, [self.lbv.dep])
        self.cp("dve", self.lbv.ap[:, 1], ex.ap[:, 1], [ex.dep], [self.lbv.dep])
        self.tt("dve", self.lbv.ap[:, 2], self.lbv.ap[:, 1], ex.ap[:, 2], ALU.add, [ex.dep, self.lbv.dep], [self.lbv.dep])
        self.tt("dve", self.lbv.ap[:, 3], self.lbv.ap[:, 2], ex.ap[:, 3], ALU.add, [ex.dep, self.lbv.dep], [self.lbv.dep])
        self.ts("dve", self.oml.ap, self.lbv.ap, -1.0, 1.0, ALU.mult, ALU.add, [self.lbv.dep], [self.oml.dep])
        self.ts("dve", self.noml.ap, self.lbv.ap, -1.0, None, ALU.add, None, [self.lbv.dep], [self.noml.dep])
        e_ = A.alloc([DEPTH * 2, KC], F32)
        s_ = A.alloc([DEPTH * 2, KC], F32)
        l1 = A.alloc([DEPTH * 2, KC], F32)
        mk = A.alloc([DEPTH * 2, KC], F32)
        lam = self.vcol("rg_lam").rearrange("p (a k) -> p a k", k=KC)
        self.act(e_.ap, lam, AF.Exp, [self.vecs.dep], [e_.dep], scale=-1.0)
        self.act(l1.ap, e_.ap, AF.Ln, [e_.dep, self.cst.dep], [l1.dep], bias=self.cst.ap[:, 1:2])
        self.ts("dve", s_.ap, e_.ap, -0.2, 0.25, ALU.mult, ALU.add, [e_.dep], [s_.dep])
        for cst in (1.0 / 3, 0.5, 1.0):
            self.tt("dve", s_.ap, s_.ap, e_.ap, ALU.mult, [s_.dep, e_.dep], [s_.dep])
            self.ts("dve", s_.ap, s_.ap, -1.0, cst, ALU.mult, ALU.add, [s_.dep], [s_.dep])
        self.tt("dve", s_.ap, s_.ap, e_.ap, ALU.mult, [s_.dep, e_.dep], [s_.dep])
        self.ts("dve", mk.ap, e_.ap, 0.03, None, ALU.is_lt, None, [e_.dep], [mk.dep])
        self.tt("dve", s_.ap, s_.ap, l1.ap, ALU.subtract, [s_.dep, l1.dep], [s_.dep])
        self.tt("dve", s_.ap, s_.ap, mk.ap, ALU.mult, [s_.dep, mk.dep], [s_.dep])
        self.tt("dve", s_.ap, s_.ap, l1.ap, ALU.add, [s_.dep, l1.dep], [s_.dep])
        self.ts("dve", self.cl.ap, s_.ap, -8.0, None, ALU.mult, None, [s_.dep], [self.cl.dep])
        self.ts("dve", self.cl2.ap, s_.ap, -16.0, None, ALU.mult, None, [s_.dep], [self.cl2.dep])
        P.barrier()
        A.reset(mark)
        self.base_mark = mark

    def rstd_from_ss(self, ss_bank, n, inv_n, rt, rstd):
        self.act(rt.ap[:, :n], ss_bank.ap[:, :n], AF.Sqrt, [ss_bank.dep, self.cst.dep], [rt.dep], scale=inv_n, bias=self.cst.ap[:, 0:1])
        ra, oa = rt.ap[:, :n], rstd.ap[:, :n]
        self.P.op("dve", lambda e: e.reciprocal(out=oa, in_=ra), [rt.dep], [rstd.dep])

    def norm_tile(self, src, b, t0, n, w, l, sc, shift_j, xt, hb, sq, rt, rstd, tmps, ssb, sqall=None):
        P = self.P
        P.dma("sp", xt.ap[:, :, :n], src[b].rearrange("k p t -> p k t")[:, :, t0:t0 + n], writes=[xt.dep])
        if sqall is not None:
            for kc in range(KC):
                self.act(sqall.ap[:, kc, :n], xt.ap[:, kc, :n], AF.Square, [xt.dep], [sqall.dep])
            for kc in range(KC):
                self.mm(ssb.ap[:, :n], self.ones.ap, sqall.ap[:, kc, :n], kc == 0, kc == KC - 1, [self.ones.dep, sqall.dep], [ssb.dep], signal=(kc == KC - 1))
        else:
            for kc in range(KC):
                s = sq[kc % 2]
                self.act(s.ap[:, :n], xt.ap[:, kc, :n], AF.Square, [xt.dep], [s.dep])
                self.mm(ssb.ap[:, :n], self.ones.ap, s.ap[:, :n], kc == 0, kc == KC - 1, [self.ones.dep, s.dep], [ssb.dep])
        self.rstd_from_ss(ssb, n, 1.0 / D, rt, rstd)
        for kc in range(KC):
            t = tmps[kc % 2]
            self.stt(t.ap[:, :n], xt.ap[:, kc, :n], sc.ap[:, l, kc, w:w + 1], rstd.ap[:, :n], ALU.mult, ALU.mult,
                     [xt.dep, sc.dep, rstd.dep], [t.dep])
            self.act(hb.ap[:, kc, :n], t.ap[:, :n], AF.Identity, [t.dep, self.modraw.dep], [hb.dep],
                     bias=self.modraw.ap[:, l * 6 + shift_j, kc, w:w + 1])

    def tiles(self, tl, skip_ctx=False):
        out = []
        for b in range(self.nb):
            for (t0, n) in tl:
                if skip_ctx and t0 < CTX:
                    continue
                out.append((b, t0, n, 2 if t0 < CTX else b))
        return out

    def phase_proj(self, l, src):
        P, A = self.P, self.A
        mark = A.mark()
        W = A.alloc([NCB, KC, 128], BF16)
        wdeps = [Dep() for _ in range(NCB)]
        order = (list(range(CB_Q, CB_Q + 8)) + list(range(CB_OG, CB_OG + 8)) +
                 list(range(CB_RGX, CB_RGX + 16)) + list(range(CB_V, CB_V + 8)) +
                 list(range(CB_FF, CB_FF + 16)) + list(range(CB_GA, CB_GA + 16)))
        for cb in order:
            P.dma("pool", W.ap[:, cb].rearrange("p k j -> p (k j)"), self.w_in[l, cb], writes=[wdeps[cb]])
        xt = A.alloc([KC, 512], F32)
        hbs = [A.alloc([KC, 512], BF16) for _ in range(2)]
        sq = [A.alloc([512], BF16) for _ in range(2)]
        rt = A.alloc([512], F32)
        rstd = A.alloc([512], F32)
        tmps = [A.alloc([512], F32) for _ in range(2)]
        evs = [A.alloc([2, 512], BF16) for _ in range(4)]
        ssb = self.banks[4]
        tiles = self.tiles(TILES512)

        def kind(cb):
            if CB_Q <= cb < CB_Q + 8 or CB_OG <= cb < CB_OG + 8:
                return "silu"
            if cb < 16 or CB_V <= cb < CB_V + 8:
                return "copy"
            return "sig"

        def norm(i):
            b, t0, n, w = tiles[i]
            self.norm_tile(src, b, t0, n, w, l, self.sc1, 0, xt, hbs[i % 2], sq, rt, rstd, tmps, ssb)
        norm(0)
        evi = 0
        for i, (b, t0, n, w) in enumerate(tiles):
            hb = hbs[i % 2]
            for bi, cb in enumerate(order):
                if bi == 16 and i + 1 < len(tiles):
                    norm(i + 1)
                bank = self.next_bank(0, 4)
                for kc in range(KC):
                    self.mm(bank.ap[:, :n], W.ap[:, cb, kc, :], hb.ap[:, kc, :n], kc == 0, kc == KC - 1,
                            [wdeps[cb], hb.dep], [bank.dep], signal=(kc == KC - 1))
                ev = evs[(evi // 2) % 4]
                g = evi % 2
                k = kind(cb)
                if k == "silu":
                    self.act(ev.ap[:, g, :n], bank.ap[:, :n], AF.Silu, [bank.dep], [ev.dep])
                elif k == "sig":
                    self.act(ev.ap[:, g, :n], bank.ap[:, :n], AF.Sigmoid, [bank.dep], [ev.dep])
                else:
                    self.cp("dve", ev.ap[:, g, :n], bank.ap[:, :n], [bank.dep], [ev.dep])
                if g == 1:
                    P.dma("sp", self.zs[b, cb - 1:cb + 1].rearrange("c p t -> p c t")[:, :, t0:t0 + n], ev.ap[:, :, :n], reads=[ev.dep])
                evi += 1
        P.barrier()
        A.reset(mark)

    def phase_rg(self, l):
        P, A = self.P, self.A
        mark = A.mark()
        ZW = NT + 6
        zx = [A.alloc([ZW], BF16) for _ in range(2)]
        zg = [A.alloc([NT], BF16) for _ in range(2)]
        for z in zx:
            self.memset("pool", z.ap, 0.0, [z.dep])
        xc = A.alloc([NT], BF16)
        Rbs = [A.alloc([NT], F32) for _ in range(2)]
        Ibs = [A.alloc([NT], F32) for _ in range(2)]
        Tb = A.alloc([NT], F32)
        B1 = A.alloc([NT], F32)
        yb = A.alloc([NT], BF16)
        dg = [A.alloc([4, 128], BF16) for _ in range(2)]
        gw = [A.alloc([4, 128], BF16) for _ in range(2)]
        for g_ in gw:
            self.memset("pool", g_.ap, 0.0, [g_.dep])

        def load(c, b, slot):
            P.dma("sp", zx[slot].ap[:, 1:1 + CTX], self.zs[b, CB_RGX + c, :, 0:CTX], writes=[zx[slot].dep])
            P.dma("sp", zx[slot].ap[:, 4 + CTX:4 + NT], self.zs[b, CB_RGX + c, :, CTX:NT], writes=[zx[slot].dep])
            P.dma("sp", zg[slot].ap, self.zs[b, CB_RGG + c], writes=[zg[slot].dep])
        seq = [(c, b) for c in range(KC) for b in range(self.nb)]
        load(seq[0][0], seq[0][1], 0)
        for si, (c, b) in enumerate(seq):
            slot = si % 2
            if b == 0:
                ws = c % 2
                for j in range(4):
                    self.ts("dve", dg[ws].ap[:, j], self.ident.ap, self.vcol(f"rg_cw{l}_{j}", c), None, ALU.mult, None,
                            [self.ident.dep, self.vecs.dep], [dg[ws].dep])
                for d in range(2):
                    for gi, src_w in ((0, self.rg_wa), (1, self.rg_wx)):
                        for hh in range(2):
                            P.dma("pool", gw[ws].ap[64 * hh:64 * hh + 64, 2 * d + gi, 64 * hh:64 * hh + 64], src_w[l, d, 2 * c + hh], writes=[gw[ws].dep])
            ws = c % 2
            if si + 1 < len(seq):
                load(seq[si + 1][0], seq[si + 1][1], 1 - slot)
            z = zx[slot]
            for (t0, n) in TILES512:
                base = t0 if t0 < CTX else t0 + 3
                bank = self.next_bank(0, 4)
                for j in range(4):
                    self.mm(bank.ap[:, :n], dg[ws].ap[:, j], z.ap[:, base + j:base + j + n], j == 0, j == 3, [dg[ws].dep, z.dep], [bank.dep], signal=(j == 3))
                self.act(xc.ap[:, t0:t0 + n], bank.ap[:, :n], AF.Identity, [bank.dep, self.vecs.dep], [xc.dep], bias=self.vcol(f"rg_cb{l}", c))
            for d in range(2):
                Tt = Tb if d == 0 else B1
                Rb, Ib = Rbs[d], Ibs[d]
                for (t0, n) in TILES512:
                    for gi, dst, bn in ((0, Rb, f"rg_ba{l}_{d}"), (1, Ib, f"rg_bx{l}_{d}")):
                        bank = self.next_bank(0, 4)
                        self.mm(bank.ap[:, :n], gw[ws].ap[:, 2 * d + gi], xc.ap[:, t0:t0 + n], True, True, [gw[ws].dep, xc.dep], [bank.dep])
                        self.act(dst.ap[:, t0:t0 + n], bank.ap[:, :n], AF.Sigmoid, [bank.dep, self.vecs.dep], [dst.dep], bias=self.vcol(bn, c))
                clc = self.cl.ap[:, l * 2 + d, c:c + 1]
                cl2c = self.cl2.ap[:, l * 2 + d, c:c + 1]
                self.act(Tt.ap, Rb.ap, AF.Exp, [Rb.dep, self.cl2.dep], [Tt.dep], scale=cl2c)
                self.act(Rb.ap, Rb.ap, AF.Exp, [Rb.dep, self.cl.dep], [Rb.dep], scale=clc)
                self.act(Tt.ap, Tt.ap, AF.Sqrt, [Tt.dep, self.cst.dep], [Tt.dep], scale=-1.0, bias=self.cst.ap[:, 1:2])
                self.tt("dve", Ib.ap, Ib.ap, Tt.ap, ALU.mult, [Ib.dep, Tt.dep], [Ib.dep])
                self.tt("dve", Ib.ap, Ib.ap, xc.ap, ALU.mult, [Ib.dep, xc.dep], [Ib.dep])
                if d == 0:
                    oa, a_, b_ = Tt.ap, Rb.ap, Ib.ap
                    P.op("dve", lambda e, oa=oa, a_=a_, b_=b_: e.tensor_tensor_scan(out=oa, data0=a_, data1=b_, initial=0.0, op0=ALU.mult, op1=ALU.add),
                         [Rb.dep, Ib.dep], [Tt.dep])
                else:
                    oa, a_, b_ = Tt.ap[:, 0:CTX][:, ::-1], Rb.ap[:, 0:CTX][:, ::-1], Ib.ap[:, 0:CTX][:, ::-1]
                    P.op("dve", lambda e, oa=oa, a_=a_, b_=b_: e.tensor_tensor_scan(out=oa, data0=a_, data1=b_, initial=0.0, op0=ALU.mult, op1=ALU.add),
                         [Rb.dep, Ib.dep], [Tt.dep])
                    oa, a_, b_ = Tt.ap[:, CTX:NT][:, ::-1], Rb.ap[:, CTX:NT][:, ::-1], Ib.ap[:, CTX:NT][:, ::-1]
                    ini = Tt.ap[:, 0:1]
                    P.op("dve", lambda e, oa=oa, a_=a_, b_=b_, ini=ini: e.tensor_tensor_scan(out=oa, data0=a_, data1=b_, initial=ini, op0=ALU.mult, op1=ALU.add),
                         [Rb.dep, Ib.dep, Tt.dep], [Tt.dep])
            if "hf" in self.dbg_out and b == 0:
                P.dma("sp", self.dbg_out["hf"][c], Tb.ap, reads=[Tb.dep])
            self.tt("dve", Tb.ap, Tb.ap, B1.ap, ALU.add, [Tb.dep, B1.dep], [Tb.dep])
            g = zg[slot]
            self.act(g.ap, g.ap, AF.Gelu_apprx_tanh, [g.dep], [g.dep])
            self.tt("dve", yb.ap, g.ap, Tb.ap, ALU.mult, [g.dep, Tb.dep], [yb.dep])
            P.dma("sp", self.yrg[b, c], yb.ap, reads=[yb.dep])
        P.barrier()
        A.reset(mark)

    def phase_hg(self, l):
        P, A = self.P, self.A
        mark = A.mark()
        HC = 32
        NCH = NT // HC
        wins = [(0, 3), (3, 3), (6, 2)] + [(8 + 3 * i, 3) for i in range(42)] + [(134, 2)]
        NW = len(wins)
        ogroups = [[0, 1, 2]] + [list(range(3 + 5 * g, 3 + 5 * g + 5)) for g in range(8)] + [[43, 44, 45]]
        cm_ = A.alloc([NT + 64], BF16)
        self.memset("pool", cm_.ap, 1.0, [cm_.dep])
        self.memset("pool", cm_.ap[:, 0:NT + 1:HC], 0.0, [cm_.dep])
        cmf = Tl(cm_.ap[:, 0:NT], cm_.dep)
        cmb = Tl(cm_.ap[:, 1:NT + 1], cm_.dep)
        zq = A.alloc([NT], BF16)
        zsg = [A.alloc([NT], BF16) for _ in range(2)]
        zv = A.alloc([NT], BF16)
        og = A.alloc([NT], BF16)
        Lb = A.alloc([NT], F32)
        Eb = A.alloc([NT], BF16)
        Kb = A.alloc([NT], BF16)
        qe = [A.alloc([NT], BF16) for _ in range(2)]
        keT = [A.alloc([NW, 128], BF16) for _ in range(2)]
        AT = [A.alloc([NW, 96], BF16) for _ in range(2)]
        ebs = [A.alloc([NCH], F32) for _ in range(2)]
        vso = Eb
        vTx = A.alloc([NW, 384], BF16)
        oacc = Lb
        Sr = [[A.alloc([128], BF16) for _ in range(8)] for _ in range(2)]
        Tr = [[A.alloc([128], BF16) for _ in range(8)] for _ in range(2)]
        sq = [A.alloc([512], BF16) for _ in range(2)]
        rsb = [A.alloc([512], F32) for _ in range(2)]
        yb = Kb
        obank = [self.banks[0], self.banks[1]]
        tbanks = [self.banks[2], self.banks[3]]
        sgb = [[self.banks[4], self.banks[5]], [self.banks[6], self.banks[7]]]
        ssbank = self.banks[2]

        def so3(ap_rm):
            return ap_rm[:, CTX:NT].rearrange("p (r c) -> p c r", c=GW)

        def nat3(ap_so):
            return ap_so[:, CTX:NT].rearrange("p (c r) -> p c r", r=GW)

        def load(hd, b):
            for dst, cb in ((zq, CB_Q), (zsg[0], CB_FF), (zsg[1], CB_FB), (zv, CB_V)):
                P.dma("sp", dst.ap, self.zs[b, cb + hd], writes=[dst.dep])
        seq = [(hd, b) for hd in range(KC) for b in range(self.nb)]
        load(seq[0][0], seq[0][1])
        tri = 0
        for si, (hd, b) in enumerate(seq):
            P.dma("sp", og.ap, self.zs[b, CB_OG + hd], writes=[og.dep])
            lbc = self.lbv.ap[:, l, hd:hd + 1]
            omlc = self.oml.ap[:, l, hd:hd + 1]
            nomlc = self.noml.ap[:, l, hd:hd + 1]
            self.cp("pool", vso.ap[:, 0:CTX], zv.ap[:, 0:CTX], [zv.dep], [vso.dep])
            self.cp("pool", nat3(vso.ap), so3(zv.ap), [zv.dep], [vso.dep])

            def transposes(src, dst):
                nonlocal tri
                for j0 in range(0, NW, 8):
                    nj = min(8, NW - j0)
                    tb = tbanks[tri % 2]
                    tri += 1
                    tbv = tb.ap.bitcast(BF16)
                    for jj in range(nj):
                        c0, ncw = wins[j0 + jj]
                        nt_ = ncw * HC
                        self.tr(tbv[:nt_, jj * 128:(jj + 1) * 128], src.ap[:, c0 * HC:c0 * HC + nt_], [src.dep], [tb.dep], signal=(jj == nj - 1))
                    if tri % 2:
                        self.cp("dve", dst.ap[:96, j0:j0 + nj, :], tbv[:96, 0:nj * 128].rearrange("p (j c) -> p j c", c=128), [tb.dep], [dst.dep])
                    else:
                        self.act(dst.ap[:96, j0:j0 + nj, :], tbv[:96, 0:nj * 128].rearrange("p (j c) -> p j c", c=128), AF.Copy, [tb.dep], [dst.dep])
            for j0 in range(0, NW, 8):
                nj = min(8, NW - j0)
                tb = tbanks[tri % 2]
                tri += 1
                tbv = tb.ap.bitcast(BF16)
                for jj in range(nj):
                    c0, ncw = wins[j0 + jj]
                    nt_ = ncw * HC
                    self.tr(tbv[:nt_, jj * 128:(jj + 1) * 128], vso.ap[:, c0 * HC:c0 * HC + nt_], [vso.dep], [tb.dep], signal=(jj == nj - 1))
                for h in range(3):
                    self.ts("dve", vTx.ap[:96, j0:j0 + nj, 128 * h:128 * h + 128], tbv[:96, 0:nj * 128].rearrange("p (j c) -> p j c", c=128),
                            self.cst.ap[:96, 2 + h:3 + h], None, ALU.mult, None, [tb.dep, self.cst.dep], [vTx.dep])
            for d in range(2):
                sg = zsg[d]
                cm = cmf if d == 0 else cmb
                self.act(Lb.ap[:, 0:CTX], sg.ap[:, 0:CTX], AF.Ln, [sg.dep, self.oml.dep, self.lbv.dep], [Lb.dep], scale=omlc, bias=lbc)
                self.act(nat3(Lb.ap), so3(sg.ap), AF.Ln, [sg.dep, self.oml.dep, self.lbv.dep], [Lb.dep], scale=omlc, bias=lbc)
                self.ts("dve", Kb.ap[:, 0:CTX], sg.ap[:, 0:CTX], nomlc, omlc, ALU.mult, ALU.add, [sg.dep, self.oml.dep, self.noml.dep], [Kb.dep])
                self.ts("dve", nat3(Kb.ap), so3(sg.ap), nomlc, omlc, ALU.mult, ALU.add, [sg.dep, self.oml.dep, self.noml.dep], [Kb.dep])
                if d == 0:
                    oa, m_, l_ = Lb.ap, cm.ap, Lb.ap
                else:
                    oa, m_, l_ = Lb.ap[:, ::-1], cm.ap[:, ::-1], Lb.ap[:, ::-1]
                P.op("dve", lambda e, oa=oa, m_=m_, l_=l_: e.tensor_tensor_scan(out=oa, data0=m_, data1=l_, initial=0.0, op0=ALU.mult, op1=ALU.add),
                     [cm.dep, Lb.dep], [Lb.dep])
                ends = Lb.ap[:, HC - 1:NT:HC] if d == 0 else Lb.ap[:, 0:NT:HC]
                self.act(ebs[d].ap, ends, AF.Exp, [Lb.dep], [ebs[d].dep])
                self.act(Eb.ap, Lb.ap, AF.Exp, [Lb.dep], [Eb.dep])
                self.tt("dve", qe[d].ap[:, 0:CTX], zq.ap[:, 0:CTX], Eb.ap[:, 0:CTX], ALU.mult, [zq.dep, Eb.dep], [qe[d].dep])
                self.tt("dve", nat3(qe[d].ap), so3(zq.ap), nat3(Eb.ap), ALU.mult, [zq.dep, Eb.dep], [qe[d].dep])
                self.act(Eb.ap, Lb.ap, AF.Exp, [Lb.dep], [Eb.dep], scale=-1.0)
                self.tt("dve", Kb.ap, Kb.ap, Eb.ap, ALU.mult, [Kb.dep, Eb.dep], [Kb.dep])
                transposes(Kb, keT[d])
                msk = self.m2f if d == 0 else self.m2b
                for j0 in range(0, NW, 5):
                    nj = min(5, NW - j0)
                    tb = tbanks[tri % 2]
                    tri += 1
                    for jj in range(nj):
                        c0, ncw = wins[j0 + jj]
                        nt_ = ncw * HC
                        tk = slice(c0 * HC, c0 * HC + nt_)
                        self.mm(tb.ap[:nt_, jj * 96:jj * 96 + nt_], Kb.ap[:, tk], qe[d].ap[:, tk], True, True,
                                [Kb.dep, qe[d].dep], [tb.dep], signal=(jj == nj - 1))
                    self.tt("dve", AT[d].ap[:96, j0:j0 + nj, :], tb.ap[:96, 0:nj * 96].rearrange("p (j c) -> p j c", c=96),
                            msk.ap[:96, 0:96].unsqueeze(1).broadcast_to([96, nj, 96]), ALU.mult, [tb.dep, msk.dep], [AT[d].dep])
            if si + 1 < len(seq):
                load(seq[si + 1][0], seq[si + 1][1])
            RS = 8
            for d in range(2):
                self.memset("pool", Sr[d][0].ap, 0.0, [Sr[d][0].dep])
            fw = [(gi, w) for gi, g in enumerate(ogroups) for w in g]
            bw = [(0, 2), (0, 1), (0, 0)] + [(gi, w) for gi in range(len(ogroups) - 1, 0, -1) for w in reversed(ogroups[gi])]
            seqs = [fw, bw]
            spos = [0, 0]
            tcnt = [0, 0]
            tmap = [dict(), dict()]
            gcount = [[0] * len(ogroups), [0] * len(ogroups)]
            oinit = [False] * len(ogroups)

            def emit_ds(d, i):
                gi, w = seqs[d][i]
                c0, ncw = wins[w]
                nt_ = ncw * HC
                Sg = sgb[d][i % 2]
                self.mm(Sg.ap[:, 0:128 * ncw], keT[d].ap[:nt_, w, :], vTx.ap[:nt_, w, 0:128 * ncw], True, True, [keT[d].dep, vTx.dep], [Sg.dep])
                for h in range(ncw):
                    ch = c0 + h
                    T = Tr[d][tcnt[d] % len(Tr[d])]
                    tcnt[d] += 1
                    tmap[d][ch] = T
                    self.act(T.ap, Sg.ap[:, 128 * h:128 * h + 128], AF.Identity, [Sg.dep, ebs[d].dep], [T.dep], scale=ebs[d].ap[:, ch:ch + 1])
            for d in range(2):
                emit_ds(d, 0)
            for i in range(len(fw)):
                for d in range(2):
                    if i + 1 < len(fw):
                        emit_ds(d, i + 1)
                    gi, w = seqs[d][i]
                    g = ogroups[gi]
                    c0, ncw = wins[w]
                    nt_ = ncw * HC
                    ob = obank[d]
                    col = (wins[w][0] - wins[g[0]][0]) * HC
                    gcount[d][gi] += 1
                    last_in_group = gcount[d][gi] == len(g)
                    hs = list(range(ncw)) if d == 0 else list(range(ncw - 1, -1, -1))
                    Sin = []
                    for h in hs:
                        ch = c0 + h
                        S = Sr[d][spos[d] % RS]
                        Sn = Sr[d][(spos[d] + 1) % RS]
                        spos[d] += 1
                        Sin.append(S)
                        T = tmap[d].pop(ch)
                        self.stt(Sn.ap, S.ap, ebs[d].ap[:, ch:ch + 1], T.ap, ALU.mult, ALU.add, [S.dep, ebs[d].dep, T.dep], [Sn.dep])
                    for hi_, h in enumerate(hs):
                        ch = c0 + h
                        S = Sin[hi_]
                        oreg = ob.ap[:, col + HC * h:col + HC * h + HC]
                        self.mm(oreg, vTx.ap[:nt_, w, 128 * h:128 * h + 128], AT[d].ap[:nt_, w, HC * h:HC * h + HC], True, False, [vTx.dep, AT[d].dep], [ob.dep], signal=False)
                        self.mm(oreg, S.ap, qe[d].ap[:, ch * HC:ch * HC + HC], False, True, [S.dep, qe[d].dep], [ob.dep], signal=True)
                    if last_in_group:
                        t0 = wins[g[0]][0] * HC
                        n = sum(wins[x][1] for x in g) * HC
                        if not oinit[gi]:
                            oinit[gi] = True
                            self.act(oacc.ap[:, t0:t0 + n], ob.ap[:, :n], AF.Identity, [ob.dep], [oacc.dep])
                        else:
                            self.tt("dve", oacc.ap[:, t0:t0 + n], oacc.ap[:, t0:t0 + n], ob.ap[:, :n], ALU.add, [ob.dep, oacc.dep], [oacc.dep])
            if "ohg" in self.dbg_out and b == 0:
                P.dma("sp", self.dbg_out["ohg"][hd], oacc.ap, reads=[oacc.dep])
            for gi, (t0, n) in enumerate(TILES512):
                s_ = sq[gi % 2]
                rs_ = rsb[gi % 2]
                self.act(s_.ap[:, :n], oacc.ap[:, t0:t0 + n], AF.Square, [oacc.dep], [s_.dep])
                self.mm(ssbank.ap[:, :n], self.ones.ap, s_.ap[:, :n], True, True, [self.ones.dep, s_.dep], [ssbank.dep])
                self.act(rs_.ap[:, :n], ssbank.ap[:, :n], AF.Ln, [ssbank.dep, self.cst.dep], [rs_.dep], scale=1.0 / 128, bias=self.cst.ap[:, 0:1])
                self.act(rs_.ap[:, :n], rs_.ap[:, :n], AF.Exp, [rs_.dep], [rs_.dep], scale=-0.5)
                self.tt("dve", oacc.ap[:, t0:t0 + n], oacc.ap[:, t0:t0 + n], rs_.ap[:, :n], ALU.mult, [oacc.dep, rs_.dep], [oacc.dep])
            gn = self.vcol(f"hg_gain{l}", hd)
            self.stt(yb.ap[:, 0:CTX], oacc.ap[:, 0:CTX], gn, og.ap[:, 0:CTX], ALU.mult, ALU.mult, [oacc.dep, og.dep, self.vecs.dep], [yb.dep])
            self.stt(yb.ap[:, CTX:NT].rearrange("p (r c) -> p r c", c=GW), oacc.ap[:, CTX:NT].rearrange("p (c r) -> p r c", r=GW), gn,
                     og.ap[:, CTX:NT].rearrange("p (r c) -> p r c", c=GW), ALU.mult, ALU.mult, [oacc.dep, og.dep, self.vecs.dep], [yb.dep])
            P.dma("sp", self.yhg[b, hd], yb.ap, reads=[yb.dep])
        P.barrier()
        A.reset(mark)

    def phase_merge(self, l, src, last):
        P, A = self.P, self.A
        mark = A.mark()
        Ws = []
        for wd in (self.w_prg, self.w_phg, self.w_out):
            W = A.alloc([KC, D], BF16)
            for kc in range(KC):
                P.dma("pool", W.ap[:, kc], wd[l, :, kc], writes=[W.dep])
            Ws.append(W)
        Wr, Wh, Wo = Ws
        ins = [[A.alloc([KC, 512], BF16) for _ in range(4)] for _ in range(2)]
        xts = [A.alloc([KC, 512], F32) for _ in range(2)]
        mb = A.alloc([KC, 512], BF16)
        ysb = A.alloc([KC, 512], F32)
        sq = [A.alloc([512], BF16) for _ in range(2)]
        t1 = [A.alloc([512], F32) for _ in range(2)]
        t2 = [A.alloc([512], F32) for _ in range(2)]
        rt = A.alloc([512], F32)
        rstd = A.alloc([512], F32)
        ssb = self.banks[7]
        tiles = self.tiles(TILES512, skip_ctx=last)

        def load(i):
            b, t0, n, w = tiles[i]
            s = i % 2
            srcs = (self.yrg[b], self.yhg[b], self.zs[b, CB_GA:CB_GA + 8], self.zs[b, CB_GB:CB_GB + 8])
            for dst, sr in zip(ins[s], srcs):
                P.dma("sp", dst.ap[:, :, :n], sr.rearrange("k p t -> p k t")[:, :, t0:t0 + n], writes=[dst.dep])
            P.dma("sp", xts[s].ap[:, :, :n], src[b].rearrange("k p t -> p k t")[:, :, t0:t0 + n], writes=[xts[s].dep])
        load(0)
        for i, (b, t0, n, w) in enumerate(tiles):
            if i + 1 < len(tiles):
                load(i + 1)
            yr, yh, ga, gb = ins[i % 2]
            xt = xts[i % 2]
            for oc in range(KC):
                ba = self.next_bank(0, 6)
                for kc in range(KC):
                    self.mm(ba.ap[:, :n], Wr.ap[:, kc, oc * 128:(oc + 1) * 128], yr.ap[:, kc, :n], kc == 0, kc == KC - 1, [Wr.dep, yr.dep], [ba.dep], signal=(kc == KC - 1))
                bb = self.next_bank(0, 6)
                for kc in range(KC):
                    self.mm(bb.ap[:, :n], Wh.ap[:, kc, oc * 128:(oc + 1) * 128], yh.ap[:, kc, :n], kc == 0, kc == KC - 1, [Wh.dep, yh.dep], [bb.dep], signal=(kc == KC - 1))
                a1, a2 = t1[oc % 2], t2[oc % 2]
                self.tt("dve", a1.ap[:, :n], ba.ap[:, :n], ga.ap[:, oc, :n], ALU.mult, [ba.dep, ga.dep], [a1.dep])
                self.tt("dve", a2.ap[:, :n], bb.ap[:, :n], gb.ap[:, oc, :n], ALU.mult, [bb.dep, gb.dep], [a2.dep])
                self.tt("dve", mb.ap[:, oc, :n], a1.ap[:, :n], a2.ap[:, :n], ALU.add, [a1.dep, a2.dep], [mb.dep])
            for oc in range(KC + 1):
                if oc < KC:
                    by = self.next_bank(0, 6)
                    for kc in range(KC):
                        self.mm(by.ap[:, :n], Wo.ap[:, kc, oc * 128:(oc + 1) * 128], mb.ap[:, kc, :n], kc == 0, kc == KC - 1, [Wo.dep, mb.dep], [by.dep], signal=(kc == KC - 1))
                    self.act(ysb.ap[:, oc, :n], by.ap[:, :n], AF.Identity, [by.dep], [ysb.dep])
                    s = sq[oc % 2]
                    self.act(s.ap[:, :n], by.ap[:, :n], AF.Square, [by.dep], [s.dep])
                if oc > 0:
                    sp = sq[(oc - 1) % 2]
                    self.mm(ssb.ap[:, :n], self.ones.ap, sp.ap[:, :n], oc == 1, oc == KC, [self.ones.dep, sp.dep], [ssb.dep])
            self.rstd_from_ss(ssb, n, 1.0 / D, rt, rstd)
            for oc in range(KC):
                a1 = t1[oc % 2]
                self.stt(a1.ap[:, :n], ysb.ap[:, oc, :n], self.gt1.ap[:, l, oc, w:w + 1], rstd.ap[:, :n], ALU.mult, ALU.mult,
                         [ysb.dep, self.gt1.dep, rstd.dep], [a1.dep])
                self.tt("pool", xt.ap[:, oc, :n], xt.ap[:, oc, :n], a1.ap[:, :n], ALU.add, [xt.dep, a1.dep], [xt.dep])
            P.dma("sp", self.xs[b].rearrange("k p t -> p k t")[:, :, t0:t0 + n], xt.ap[:, :, :n], reads=[xt.dep])
        P.barrier()
        A.reset(mark)

    def phase_ffn(self, l, last):
        P, A = self.P, self.A
        mark = A.mark()
        N = 256
        Wu = A.alloc([KC, 2 * DFF], BF16)
        for kc in range(KC):
            for q in range(4):
                P.dma("pool", Wu.ap[:, kc, q * 1408:(q + 1) * 1408], self.w_up[l, :, kc, q * 1408:(q + 1) * 1408], writes=[Wu.dep])
        Wd = A.alloc([FFC, D], BF16)
        for fc in range(FFC):
            P.dma("pool", Wd.ap[:, fc], self.w_dn[l, :, fc], writes=[Wd.dep])
        dgr = [A.alloc([3, 128], BF16) for _ in range(4)]
        xts = [A.alloc([KC, N], F32) for _ in range(2)]
        hbs = [A.alloc([KC, N], BF16) for _ in range(2)]
        sq = [A.alloc([N], BF16) for _ in range(2)]
        rt = A.alloc([N], F32)
        rstd = A.alloc([N], F32)
        rt2 = A.alloc([N], F32)
        rstd2 = A.alloc([N], F32)
        tmps = [A.alloc([N], F32) for _ in range(2)]
        gp = [A.alloc([4, GW + 2], BF16) for _ in range(2)]
        gpc = [A.alloc([N + 2], BF16) for _ in range(2)]
        for g_ in gp + gpc:
            self.memset("pool", g_.ap, 0.0, [g_.dep])
        sb = [A.alloc([N], BF16) for _ in range(2)]
        ab = A.alloc([FFC, N], BF16)
        ysb = A.alloc([KC, N], F32)
        ssb = self.banks[7]
        tiles = self.tiles(TILES256, skip_ctx=last)

        def norm(i):
            b, t0, n, w = tiles[i]
            self.norm_tile(self.xs, b, t0, n, w, l, self.sc2, 3, xts[i % 2], hbs[i % 2], sq, rt, rstd, tmps, ssb)
        norm(0)
        for i, (b, t0, n, w) in enumerate(tiles):
            hb = hbs[i % 2]
            xr = xts[i % 2]
            isctx = t0 < CTX
            gbank = {}

            def emit_g(fc):
                dgs = dgr[fc % 4]
                for j in range(3):
                    self.ts("dve", dgs.ap[:, j], self.ident.ap, self.vcol(f"ffn_cw{l}_{j}", fc), None, ALU.mult, None,
                            [self.ident.dep, self.vecs.dep], [dgs.dep])
                bg = self.next_bank(0, 6)
                gbank[fc] = bg
                for kc in range(KC):
                    self.mm(bg.ap[:, :N], Wu.ap[:, kc, fc * 128:(fc + 1) * 128], hb.ap[:, kc, :], kc == 0, kc == KC - 1, [Wu.dep, hb.dep], [bg.dep], signal=(kc == KC - 1))

            def emit_rest(fc):
                dgs = dgr[fc % 4]
                bg = gbank.pop(fc)
                bc = self.next_bank(0, 6)
                if isctx:
                    g_ = gpc[fc % 2]
                    self.act(g_.ap[:, 1:N + 1], bg.ap[:, :N], AF.Identity, [bg.dep], [g_.dep])
                    for j in range(3):
                        self.mm(bc.ap[:, :N], dgs.ap[:, j], g_.ap[:, j:j + N], j == 0, j == 2, [dgs.dep, g_.dep], [bc.dep], signal=(j == 2))
                else:
                    g_ = gp[fc % 2]
                    self.act(g_.ap[:, :, 1:GW + 1], bg.ap[:, :N].rearrange("p (r c) -> p r c", c=GW), AF.Identity, [bg.dep], [g_.dep])
                    for j in range(3):
                        self.mm(bc.ap[:, :N].rearrange("p (r c) -> p r c", c=GW), dgs.ap[:, j], g_.ap[:, :, j:j + GW], j == 0, j == 2, [dgs.dep, g_.dep], [bc.dep], signal=(j == 2))
                bv = self.next_bank(0, 6)
                for kc in range(KC):
                    self.mm(bv.ap[:, :N], Wu.ap[:, kc, DFF + fc * 128:DFF + (fc + 1) * 128], hb.ap[:, kc, :], kc == 0, kc == KC - 1, [Wu.dep, hb.dep], [bv.dep], signal=(kc == KC - 1))
                s_ = sb[fc % 2]
                self.act(s_.ap, bc.ap[:, :N], AF.Silu, [bc.dep, self.vecs.dep], [s_.dep], bias=self.vcol(f"ffn_cb{l}", fc))
                self.tt("dve", ab.ap[:, fc, :], s_.ap, bv.ap[:, :N], ALU.mult, [s_.dep, bv.dep], [ab.dep])
            emit_g(0)
            for fc in range(FFC):
                if fc + 1 < FFC:
                    emit_g(fc + 1)
                emit_rest(fc)
            if i + 1 < len(tiles):
                norm(i + 1)
            for oc in range(KC):
                by = self.next_bank(0, 6)
                for fc in range(FFC):
                    self.mm(by.ap[:, :N], Wd.ap[:, fc, oc * 128:(oc + 1) * 128], ab.ap[:, fc, :], fc == 0, fc == FFC - 1, [Wd.dep, ab.dep], [by.dep], signal=(fc == FFC - 1))
                self.act(ysb.ap[:, oc, :], by.ap[:, :N], AF.Identity, [by.dep], [ysb.dep])
                s = sq[oc % 2]
                self.act(s.ap[:, :N], by.ap[:, :N], AF.Square, [by.dep], [s.dep])
                self.mm(self.banks[6].ap[:, :N], self.ones.ap, s.ap[:, :N], oc == 0, oc == KC - 1, [self.ones.dep, s.dep], [self.banks[6].dep])
            self.rstd_from_ss(self.banks[6], N, 1.0 / D, rt2, rstd2)
            for oc in range(KC):
                self.stt(ysb.ap[:, oc, :], ysb.ap[:, oc, :], self.gt2.ap[:, l, oc, w:w + 1], rstd2.ap[:, :N], ALU.mult, ALU.mult,
                         [ysb.dep, self.gt2.dep, rstd2.dep], [ysb.dep])
                self.tt("pool", xr.ap[:, oc, :], xr.ap[:, oc, :], ysb.ap[:, oc, :], ALU.add, [xr.dep, ysb.dep], [xr.dep])
            if last:
                P.dma("sp", self.out[b].rearrange("k p t -> p k t")[:, :, t0 - CTX:t0 - CTX + n], xr.ap, reads=[xr.dep])
            else:
                P.dma("sp", self.xs[b].rearrange("k p t -> p k t")[:, :, t0:t0 + n], xr.ap, reads=[xr.dep])
        P.barrier()
        A.reset(mark)

    def build(self):
        nc = self.nc
        for name in self.dbg:
            if name == "hf":
                self.dbg_out["hf"] = nc.dram_tensor("hf", [KC, 128, NT], F32, kind="ExternalOutput").ap()
            if name == "ohg":
                self.dbg_out["ohg"] = nc.dram_tensor("ohg", [KC, 128, NT], F32, kind="ExternalOutput").ap()
            if name == "mod":
                self.dbg_out["mod"] = nc.dram_tensor("mod", [128, DEPTH * 6 * KC * 3], F32, kind="ExternalOutput").ap()
                self.dbg_out["cl"] = nc.dram_tensor("cl", [128, DEPTH * 2 * KC], F32, kind="ExternalOutput").ap()
                self.dbg_out["lb"] = nc.dram_tensor("lb", [128, DEPTH * KC], F32, kind="ExternalOutput").ap()
        self.phase0()
        if "mod" in self.dbg_out:
            self.P.dma("sp", self.dbg_out["mod"], self.modraw.ap.rearrange("p a k w -> p (a k w)"), reads=[self.modraw.dep])
            self.P.dma("sp", self.dbg_out["cl"], self.cl.ap.rearrange("p a k -> p (a k)"), reads=[self.cl.dep])
            self.P.dma("sp", self.dbg_out["lb"], self.lbv.ap.rearrange("p a k -> p (a k)"), reads=[self.lbv.dep])
        for l in self.layers:
            last = (l == DEPTH - 1)
            src = self.xin if l == 0 else self.xs
            if "P" in self.phases:
                self.phase_proj(l, src)
            if "R" in self.phases:
                self.phase_rg(l)
            if "H" in self.phases:
                self.phase_hg(l)
            if "C" in self.phases:
                self.phase_merge(l, src, last)
            if "D" in self.phases:
                self.phase_ffn(l, last)
        self.P.barrier()
        self.P.finish()
        return nc


def fm(v, ncol):
    return np.ascontiguousarray(np.asarray(v, np.float32).reshape(ncol, 128).T)


def pack_vecs(inp, core, nb=NB):
    V = np.zeros((128, NV), np.float32)

    def put(name, arr):
        o, n = VOFF[name]
        V[:, o:o + n] = arr
    for l in range(DEPTH):
        for n in ("g_pre_mix", "g_post_mix", "g_pre_ffn", "g_post_ffn"):
            put(f"{n}{l}", fm(inp[n][l], 8))
        for j in range(4):
            put(f"rg_cw{l}_{j}", fm(inp["rg_conv_w"][l, j], 8))
        put(f"rg_cb{l}", fm(inp["rg_conv_b"][l], 8))
        for d in range(2):
            put(f"rg_ba{l}_{d}", fm(inp["rg_ba"][l, d], 8))
            put(f"rg_bx{l}_{d}", fm(inp["rg_bx"][l, d], 8))
        put(f"hg_gain{l}", fm(inp["hg_out_norm"][l], 8))
        for j in range(3):
            put(f"ffn_cw{l}_{j}", fm(inp["ffn_conv_w"][l, j], FFC))
        put(f"ffn_cb{l}", fm(inp["ffn_conv_b"][l], FFC))
        put(f"b_ada{l}", fm(inp["b_ada"][l], 48))
    lam = np.concatenate([fm(inp["rg_lam"][l, d], 8) for l in range(DEPTH) for d in range(2)], axis=1)
    put("rg_lam", lam)
    put("hg_lb", np.concatenate([fm(inp["hg_lb_logits"][l], 8) for l in range(DEPTH)], axis=1))
    c3 = np.zeros((128, 8, 3), np.float32)
    for w in range(nb):
        c3[:, :, w] = fm(inp["c"][core * nb + w], 8)
    c3[:, :, 2] = fm(inp["c_ctx"], 8)
    put("c3", c3.reshape(128, 24))
    return V


_SHARED = {}


def shared_weights(inp):
    key = id(inp["w_in"])
    if _SHARED.get("key") == key:
        return _SHARED["val"]
    f = lambda a: np.ascontiguousarray(np.asarray(a, np.float32))
    w = {}
    w["w_ada"] = f(np.asarray(inp["w_ada"]).reshape(DEPTH, KC, 128, 6, D).transpose(0, 3, 2, 1, 4))
    w["w_in"] = f(np.asarray(inp["w_in"]).reshape(DEPTH, KC, 128, NCB, 128).transpose(0, 3, 2, 1, 4)).reshape(DEPTH, NCB, 128, KC * 128)
    w["rg_wa"] = f(inp["rg_wa"])
    w["rg_wx"] = f(inp["rg_wx"])
    for n, k in (("w_prg", "w_proj_rg"), ("w_phg", "w_proj_hg"), ("w_out", "w_out")):
        w[n] = f(np.asarray(inp[k]).reshape(DEPTH, KC, 128, D).transpose(0, 2, 1, 3))
    w["w_up"] = f(np.asarray(inp["ffn_w_up"]).reshape(DEPTH, KC, 128, 2 * DFF).transpose(0, 2, 1, 3))
    w["w_dn"] = f(np.asarray(inp["ffn_w_down"]).reshape(DEPTH, FFC, 128, D).transpose(0, 2, 1, 3))
    _SHARED["key"] = key
    _SHARED["val"] = w
    return w


def core_inputs(inp, core, nb=NB):
    w = shared_weights(inp)
    xin = np.empty((nb, KC, 128, NT), np.float32)
    for i in range(nb):
        bi = core * nb + i
        xin[i, :, :, :CTX] = np.asarray(inp["ctx"][bi]).T.reshape(KC, 128, CTX)
        xin[i, :, :, CTX:] = np.asarray(inp["x"][bi]).T.reshape(KC, 128, LAT)
    m = dict(w)
    m["xin"] = xin
    m["vecs"] = pack_vecs(inp, core, nb)
    return m


_NC_CACHE = {}


def kernel(**inputs):
    inp = {k: np.asarray(v) for k, v in inputs.items()}
    if "nc" not in _NC_CACHE:
        _NC_CACHE["nc"] = Builder().build()
    nc = _NC_CACHE["nc"]
    in_maps = [core_inputs(inp, c) for c in range(NCORES)]
    res = run_bass_kernel_spmd(nc, in_maps, core_ids=list(range(NCORES)))
    out = np.empty((NCORES * NB, LAT, D), np.float32)
    for c in range(NCORES):
        o = res.results[c]["out"]
        for i in range(NB):
            out[c * NB + i] = o[i].reshape(D, LAT).T
    return out
```
